# Optimizing a Trainium2 kernel written in Bass

```python
import jax, jax.numpy as jnp
from jax import lax
import numpy as np

D_MODEL = 1024
BATCH = 2
SEQ = 8192
DEPTH = 1
DEC_BATCH = 16
DEC_SEQ = 16
PAST_LEN = 1024

CHUNK = 64
N_META = 16
H_A = 8
DH = 64
W_A = H_A * DH
N_CG = 8
W_C = D_MODEL - W_A
CONV_W = 3
Q_BLOCK = 128
SPLIT_SIZES = [W_A, W_A, W_A, W_A, H_A, W_C, W_C, W_C, W_C]
N_IN = sum(SPLIT_SIZES)
SPLIT_IDX = [int(i) for i in np.cumsum(SPLIT_SIZES)[:-1]]
EPS = 1e-6

kernel_name = "hymba_fox_shortconv_stream_step"


def rmsnorm(x, g):
    xf = x.astype(jnp.float32)
    y = xf * lax.rsqrt(jnp.mean(xf * xf, axis=-1, keepdims=True) + EPS)
    return (y * g.astype(jnp.float32)).astype(x.dtype)


def project(h, w_in, b_f):
    p = jnp.einsum('bld,dn->bln', h, w_in)
    q, k, v, za, fl, bg, cg, hc, zc = jnp.split(p, SPLIT_IDX, axis=-1)
    B, L = h.shape[:2]
    q = q.reshape(B, L, H_A, DH)
    k = k.reshape(B, L, H_A, DH)
    v = v.reshape(B, L, H_A, DH)
    logf = jax.nn.log_sigmoid((fl + b_f).astype(jnp.float32))
    u = cg * hc
    return q, k, v, za, logf, bg, u, zc


def fox_block(qblk, cq, qpos, k, v, ck, kpos):
    s = jnp.einsum('bqhd,bkhd->bhqk', qblk, k, preferred_element_type=jnp.float32) * (DH ** -0.5)
    s = s + jnp.transpose(cq, (0, 2, 1))[..., None] - jnp.transpose(ck, (0, 2, 1))[:, :, None, :]
    mask = kpos[None, :] <= qpos[:, None]
    s = jnp.where(mask[None, None], s, -jnp.inf)
    p = jax.nn.softmax(s, axis=-1)
    return jnp.einsum('bhqk,bkhd->bqhd', p.astype(v.dtype), v)


def fox_prompt_attention(q, k, v, logf):
    B, L = q.shape[:2]
    c = jnp.cumsum(logf, axis=1)
    nb = -(-L // Q_BLOCK)
    pad = nb * Q_BLOCK - L
    qb = jnp.pad(q, ((0, 0), (0, pad), (0, 0), (0, 0))).reshape(B, nb, Q_BLOCK, H_A, DH)
    qb = jnp.transpose(qb, (1, 0, 2, 3, 4))
    cb = jnp.pad(c, ((0, 0), (0, pad), (0, 0))).reshape(B, nb, Q_BLOCK, H_A)
    cb = jnp.transpose(cb, (1, 0, 2, 3))
    starts = jnp.arange(nb, dtype=jnp.int32) * Q_BLOCK
    kpos = jnp.arange(L, dtype=jnp.int32)

    def one_block(args):
        qblk, cblk, s0 = args
        qpos = s0 + jnp.arange(Q_BLOCK, dtype=jnp.int32)
        return fox_block(qblk, cblk, qpos, k, v, c, kpos)

    ob = lax.map(one_block, (qb, cb, starts))
    return jnp.transpose(ob, (1, 0, 2, 3, 4)).reshape(B, nb * Q_BLOCK, H_A, DH)[:, :L]


def causal_conv(u_full, w):
    L = u_full.shape[1] - (CONV_W - 1)
    y = w[0] * u_full[:, 0:L]
    for i in range(1, CONV_W):
        y = y + w[i] * u_full[:, i:i + L]
    return y


def merge(o_a, za, y_c, zc, w_out):
    B, L = o_a.shape[:2]
    cat = jnp.concatenate([o_a.reshape(B, L, W_A) * jax.nn.silu(za), y_c * jax.nn.silu(zc)], axis=-1)
    return jnp.einsum('blm,md->bld', cat, w_out)


def setup_inputs(seed: int = 0) -> dict:
    key = jax.random.key(seed)
    ks = jax.random.split(key, 16)
    f32 = jnp.float32
    x_prompt = jax.random.normal(ks[0], (BATCH, SEQ, D_MODEL), f32)
    x_sample = jax.random.normal(ks[1], (DEC_BATCH, DEC_SEQ, D_MODEL), f32)
    cache_k = jax.random.normal(ks[2], (DEPTH, DEC_BATCH, PAST_LEN, H_A, DH), f32)
    cache_v = jax.random.normal(ks[3], (DEPTH, DEC_BATCH, PAST_LEN, H_A, DH), f32)
    cache_logf = jax.nn.log_sigmoid(1.0 + jax.random.normal(ks[4], (DEPTH, DEC_BATCH, PAST_LEN, H_A), f32))
    state_conv = jax.random.normal(ks[5], (DEPTH, DEC_BATCH, CONV_W - 1, W_C), f32)
    meta_tokens = jax.random.normal(ks[6], (N_META, D_MODEL), f32)
    norm_g = 1.0 + 0.02 * jax.random.normal(ks[7], (DEPTH, D_MODEL), f32)
    w_in = jax.random.normal(ks[8], (DEPTH, D_MODEL, N_IN), f32) * D_MODEL ** -0.5
    b_f = 1.0 + 0.1 * jax.random.normal(ks[9], (DEPTH, H_A), f32)
    conv_w = jax.random.normal(ks[10], (DEPTH, CONV_W, W_C), f32) * CONV_W ** -0.5
    w_out = jax.random.normal(ks[11], (DEPTH, D_MODEL, D_MODEL), f32) * D_MODEL ** -0.5
    final_g = 1.0 + 0.02 * jax.random.normal(ks[12], (D_MODEL,), f32)
    return {"x_prompt": x_prompt, "x_sample": x_sample, "cache_k": cache_k, "cache_v": cache_v,
            "cache_logf": cache_logf, "state_conv": state_conv, "meta_tokens": meta_tokens,
            "norm_g": norm_g, "w_in": w_in, "b_f": b_f, "conv_w": conv_w, "w_out": w_out,
            "final_g": final_g}


def reference(x_prompt, x_sample, cache_k, cache_v, cache_logf, state_conv, meta_tokens,
              norm_g, w_in, b_f, conv_w, w_out, final_g):
    B = x_prompt.shape[0]
    meta = jnp.broadcast_to(meta_tokens.astype(x_prompt.dtype)[None], (B, N_META, D_MODEL))
    xp = jnp.concatenate([meta, x_prompt], axis=1)
    kp_l, vp_l, fp_l, cp_l = [], [], [], []
    for l in range(DEPTH):
        h = rmsnorm(xp, norm_g[l])
        q, k, v, za, logf, bg, u, zc = project(h, w_in[l], b_f[l])
        o_a = fox_prompt_attention(q, k, v, logf)
        u_full = jnp.pad(u, ((0, 0), (CONV_W - 1, 0), (0, 0)))
        y_c = bg * causal_conv(u_full, conv_w[l])
        xp = xp + merge(o_a, za, y_c, zc, w_out[l])
        kp_l.append(k); vp_l.append(v); fp_l.append(logf); cp_l.append(u[:, -(CONV_W - 1):])
    y_prompt = rmsnorm(xp, final_g)[:, N_META:]

    xs = x_sample
    S = xs.shape[1]
    P = cache_k.shape[2]
    ks_l, vs_l, fs_l, cs_l = [], [], [], []
    for l in range(DEPTH):
        h = rmsnorm(xs, norm_g[l])
        q, k, v, za, logf, bg, u, zc = project(h, w_in[l], b_f[l])
        k_all = jnp.concatenate([cache_k[l].astype(k.dtype), k], axis=1)
        v_all = jnp.concatenate([cache_v[l].astype(v.dtype), v], axis=1)
        c_all = jnp.cumsum(jnp.concatenate([cache_logf[l].astype(jnp.float32), logf], axis=1), axis=1)
        kpos = jnp.arange(P + S, dtype=jnp.int32)
        qpos = P + jnp.arange(S, dtype=jnp.int32)
        o_a = fox_block(q, c_all[:, P:], qpos, k_all, v_all, c_all, kpos)
        u_full = jnp.concatenate([state_conv[l].astype(u.dtype), u], axis=1)
        y_c = bg * causal_conv(u_full, conv_w[l])
        xs = xs + merge(o_a, za, y_c, zc, w_out[l])
        ks_l.append(k); vs_l.append(v); fs_l.append(logf); cs_l.append(u_full[:, -(CONV_W - 1):])
    y_sample = rmsnorm(xs, final_g)

    return (y_prompt, y_sample,
            jnp.stack(kp_l), jnp.stack(vp_l), jnp.stack(fp_l), jnp.stack(cp_l),
            jnp.stack(ks_l), jnp.stack(vs_l), jnp.stack(fs_l), jnp.stack(cs_l))
```

```python
import numpy as np
import concourse.bass as bass
import concourse.mybir as mybir
from concourse.bass_utils import run_bass_kernel_spmd

F32 = mybir.dt.float32
BF16 = mybir.dt.bfloat16
AF = mybir.ActivationFunctionType
ALU = mybir.AluOpType
ENGINES = ("pe", "act", "dve", "pool", "sp")
D = 1024
KC = 8
H = 8
EPS = 1e-6
NEG = -30000.0


class _Rec:
    def __getattr__(self, name):
        return lambda *a, **k: (name, a, k)


_REC = _Rec()


class Prog:
    def __init__(self, nc, n_dma_sems=8):
        self.nc = nc
        self.ops = []
        self.last_write = {}
        self.readers = {}
        self.n_dma_sems = n_dma_sems
        self.marks = {}
        self.real_write = {}

    def mark(self, name):
        self.marks[name] = len(self.ops)

    def op(self, eng, fn, reads=(), writes=(), dma=False):
        i = len(self.ops)
        deps = set()
        rdps = [r for r in reads if r.startswith("ps") and r not in writes]
        writes = list(writes) + rdps
        for r in reads:
            if r in self.last_write:
                d = self.last_write[r]
                if r in rdps and r in self.ops[d]["rdps"] and self.ops[d]["eng"] == eng and not dma and not self.ops[d]["dma"]:
                    if r in self.real_write:
                        deps.add(self.real_write[r])
                else:
                    deps.add(d)
        for w in writes:
            if w in self.last_write:
                d = self.last_write[w]
                if w in rdps and w in self.ops[d]["rdps"] and self.ops[d]["eng"] == eng and not dma and not self.ops[d]["dma"]:
                    if w in self.real_write:
                        deps.add(self.real_write[w])
                else:
                    deps.add(d)
            for rd in self.readers.get(w, ()):
                deps.add(rd)
        for w in writes:
            if w not in rdps:
                self.real_write[w] = i
        self.ops.append(dict(eng=eng, fn=fn(_REC), deps=deps, dma=dma, rdps=set(rdps)))
        for r in reads:
            self.readers.setdefault(r, []).append(i)
        for w in writes:
            self.last_write[w] = i
            self.readers[w] = []
        return i

    def emit(self):
        nc = self.nc
        ops = self.ops
        needed = [False] * len(ops)
        for i, o in enumerate(ops):
            for d in o["deps"]:
                p = ops[d]
                if p["eng"] == "pe" and o["eng"] == "pe" and not p["dma"] and not o["dma"]:
                    continue
                needed[d] = True
        csem = {e: nc.alloc_semaphore(name=f"c_{e}") for e in ENGINES}
        dsem = {e: [nc.alloc_semaphore(name=f"d_{e}{k}") for k in range(self.n_dma_sems)] for e in ENGINES
                if any(o["dma"] and o["eng"] == e for o in ops)}
        ccount = {e: 0 for e in ENGINES}
        dcount = {e: [0] * self.n_dma_sems for e in dsem}
        drr = {e: 0 for e in dsem}
        sig = [None] * len(ops)
        pre_wait = [None] * len(ops)
        for i, o in enumerate(ops):
            e = o["eng"]
            if o["dma"]:
                k = drr[e]
                drr[e] = (k + 1) % self.n_dma_sems
                prev = dcount[e][k]
                if prev > 0:
                    pre_wait[i] = (dsem[e][k], prev)
                dcount[e][k] = prev + 16
                sig[i] = (dsem[e][k], prev + 16)
            elif needed[i]:
                ccount[e] += 1
                sig[i] = (csem[e], ccount[e])
        streams = {e: [] for e in ENGINES}
        waited = {e: {} for e in ENGINES}
        for i, o in enumerate(ops):
            e = o["eng"]
            cand = []
            if pre_wait[i] is not None:
                cand.append(pre_wait[i])
            for d in sorted(o["deps"]):
                if sig[d] is None:
                    continue
                p = ops[d]
                if p["eng"] == "pe" and e == "pe" and not p["dma"] and not o["dma"]:
                    continue
                cand.append(sig[d])
            best = {}
            for (s, v) in cand:
                key = id(s)
                if key not in best or best[key][1] < v:
                    best[key] = (s, v)
            waits = []
            for key, (s, v) in best.items():
                if waited[e].get(key, 0) >= v:
                    continue
                waited[e][key] = v
                waits.append((s, v))
            streams[e].append((i, waits))
        final_waits = []
        for e in dsem:
            for k in range(self.n_dma_sems):
                if dcount[e][k] > 0:
                    final_waits.append((dsem[e][k], dcount[e][k]))
        self.stats = dict(n_ops=len(ops), n_sig=sum(1 for s in sig if s is not None),
                          n_waits=sum(len(w) for e in ENGINES for _, w in streams[e]))

        def run(e, eng):
            for (i, waits) in streams[e]:
                for (s, v) in waits:
                    eng.wait_ge(s, v)
                nm_, a_, k_ = ops[i]["fn"]
                ins = getattr(eng, nm_)(*a_, **k_)
                if sig[i] is not None:
                    ins.then_inc(sig[i][0], 16 if ops[i]["dma"] else 1)
            if e == "sp":
                for (s, v) in final_waits:
                    eng.wait_ge(s, v)

        with nc.Block() as block:
            @block.tensor
            def _(eng):
                run("pe", eng)

            @block.scalar
            def _(eng):
                run("act", eng)

            @block.vector
            def _(eng):
                run("dve", eng)

            @block.gpsimd
            def _(eng):
                run("pool", eng)

            @block.sync
            def _(eng):
                run("sp", eng)


def build(NB=64, SBC=2, P=1024, do_sample=True, upto=None):
    NG = NB // 4
    NSB = NG // 4
    NBLK1 = NB + 1
    TT = 16 + NB * 128
    NH = 2 * NG + 2
    PB = P // 128
    NS = SBC * 16
    nc = bass.Bass("TRN2", target_bir_lowering=False)
    pg = Prog(nc)

    def din(name, shape, dt=F32):
        return nc.dram_tensor(name, list(shape), dt, kind="ExternalInput").ap()

    def dout(name, shape, dt=F32):
        return nc.dram_tensor(name, list(shape), dt, kind="ExternalOutput").ap()

    xa = din("xa", [NB * 128, D]); meta = din("meta", [16, D]); xh = din("xh", [NH, D])
    w1 = din("w1", [128, KC, 1032]); w2 = din("w2", [128, KC, 3072])
    woa = din("woa", [64, H, D]); woc = din("woc", [128, 4, D])
    gcol = din("gcol", [128, KC]); fg = din("fg", [128, D]); bfi = din("bf", [128, 4, H])
    cwi = din("cw", [128, 4, 3]); cst = din("cst", [128, 3 * 128]); mski = din("msk", [128, 4 * 128])
    w4i = din("w4", [128, 16]); mrow = din("mrow", [128, 1])
    y_own = dout("y_own", [NG * 128, D]); nk_own = dout("nk_own", [NG * 128, 512])
    nv_own = dout("nv_own", [NG * 128, 512]); nlf_own = dout("nlf_own", [NG * 128, H])
    nk_m = dout("nk_m", [16, 512]); nv_m = dout("nv_m", [16, 512]); nlf_m = dout("nlf_m", [16, H])
    ncv = dout("ncv", [2, 512])
    if do_sample:
        xs = din("xs", [NS, D]); cki = din("ck", [SBC, P, 512]); cvi = din("cv", [SBC, P, 512])
        clf = din("clf", [SBC, P, H]); scT = din("scT", [128, 4, SBC, 2])
        ys = dout("ys", [NS, D]); nks = dout("nks", [NS, 512]); nvs = dout("nvs", [NS, 512])
        nlfs = dout("nlfs", [NS, H]); ncs = dout("ncs", [SBC, 2, 512])
    kT_scr = nc.dram_tensor("kT_scr", [4, 128, TT], BF16, kind="Internal").ap()
    kaug_scr = nc.dram_tensor("kaug_scr", [H * 3, TT], BF16, kind="Internal").ap()
    v_scr = nc.dram_tensor("v_scr", [4, 128, NBLK1, 132], BF16, kind="Internal").ap()

    def sb(name, shape, dt=F32):
        return nc.alloc_sbuf_tensor(name, list(shape), dt)

    psA = [nc.alloc_psum_tensor(f"psA{i}", [128, 512], F32) for i in range(2)]
    psSS = [nc.alloc_psum_tensor(f"psSS{i}", [128, 1024], F32) for i in range(2)]
    psS = [psSS[0][:, 0:512], psSS[0][:, 512:1024], psSS[1][:, 0:512]]
    psT = psSS[1][:, 512:1024].bitcast(BF16).rearrange("p (c t) -> p c t", c=KC)
    SNAMES = [["psS0", "psS1"], ["psS2", "psT"]]
    psO = [nc.alloc_psum_tensor(f"psO{i}", [128, 512], F32) for i in range(2)]
    W1b = sb("W1b", [128, KC, 1032], BF16); W2b = sb("W2b", [128, KC, 3072], BF16)
    woab = sb("woab", [128, H, D], BF16); wocb = sb("wocb", [128, 4, D], BF16)
    gct = sb("gct", [128, KC]); fgt = sb("fgt", [128, D]); bft = sb("bft", [128, 4, H]); cwt = sb("cwt", [128, 4, 3])
    cstt = sb("cstt", [128, 384]); mskt = sb("mskt", [128, 512]); w4t = sb("w4t", [128, 16]); mrt = sb("mrt", [128, 1])
    identb = sb("identb", [128, 128], BF16); onesb = sb("onesb", [128, 128], BF16)
    maskb = sb("maskb", [128, 4, 128], BF16); trib = sb("trib", [128, 128], BF16)
    xt = [sb(f"xt{i}", [128, D]) for i in range(3)]
    xsb = [sb(f"xsb{i}", [128, D], BF16) for i in range(2)]
    ssq = sb("ssq", [128, 8]); rst = sb("rst", [128, 8])
    hT = [sb(f"hT{i}", [128, KC, 512], BF16) for i in range(2)]
    hTx = sb("hTx", [128, KC, 64], BF16)
    o32 = [sb("o320", [128, 512])] * 2
    NC = sb("NC", [128, NBLK1, H]); CPt = sb("CPt", [128, NG, H, 3], BF16)
    uh = sb("uh", [128, 4, NH])
    KCH = 8
    KCOL = 16 + KCH * 128
    kbuf = [[sb(f"kb{i}{j}", [128, KCOL], BF16) for j in range(2)] for i in range(2)]
    vbuf = [sb(f"vb{i}", [128, KCH + 1, 132], BF16) for i in range(2)]
    pTT = [sb("pTT0", [128, 2, 512], BF16), hT[1][:, 0:2, :]]
    PNAMES = ["pT0", "pT2"]
    rr = sb("rr", [128, D])
    o3 = [(o32[0][:], "o320"), (rr[:, 0:512], "rr0"), (rr[:, 512:1024], "rr1")]
    nb8 = NBLK1 * H
    NREG = max(2048 + 2112 + 264 + 2064 + 5 * nb8 + 2 * (NG + 1) * H + 2 * NG * H, 3 * 2048 + 1024 + 3 * 512 + 520 + 512) + 64
    REG = sb("REG", [128, NREG])
    _o = [0]

    def rv(nwords, dt=F32, pattern=None, **kw):
        a = REG[:, _o[0]:_o[0] + nwords]
        _o[0] += nwords
        if dt == BF16:
            a = a.bitcast(BF16)
        if pattern:
            a = a.rearrange(pattern, **kw)
        return a

    ktsb = [rv(1024, BF16, "p (a b) -> p a b", a=4) for i in range(2)]
    vsb = [rv(1056, BF16, "p (a b c d) -> p a b c d", a=4, b=4, c=2) for i in range(2)]
    vmt = rv(264, BF16, "p (a b c d) -> p a b c d", a=4, b=1, c=2)
    wst = [rv(1032) for i in range(2)]
    Z = rv(nb8, F32, "p (b h) -> p b h", h=H); LF = rv(nb8, F32, "p (b h) -> p b h", h=H)
    CL = rv(nb8, F32, "p (b h) -> p b h", h=H); TOT = rv(nb8, F32, "p (b h) -> p b h", h=H)
    BP = rv(nb8, F32, "p (b h) -> p b h", h=H)
    sa = [rv((NG + 1) * H, F32, "p (b h) -> p b h", h=H) for i in range(2)]
    c8 = rv(NG * H, F32, "p (b h) -> p b h", h=H); r1 = rv(NG * H, F32, "p (b h) -> p b h", h=H)
    P1_NAMES = ["ktsb0", "ktsb1", "ktsb1.1", "ktsb1.2", "ktsb1.3", "vsb0", "vsb1", "vmt", "wst0", "wst1", "Z", "LF", "CL", "TOT", "BP", "sa0", "sa1", "c8", "r1"]
    _o[0] = 0
    qa = rv(2048, BF16, "p (a b) -> p a b", a=H); sz = rv(2048, BF16, "p (a b) -> p a b", a=H)
    ca = rv(2048, BF16, "p (a b) -> p a b", a=H); catc = rv(1024, BF16, "p (a b) -> p a b", a=4)
    t5 = [rv(512) for i in range(3)]
    ut = rv(520, F32, "p (a b) -> p a b", a=4); acc = rv(512)
    rdh_v = t5[1][:, 0:256].bitcast(BF16); rdl_v = t5[1][:, 256:512].bitcast(BF16)
    P2_NAMES = ["qa", "sz", "ca", "catc", "t50", "t51", "t52", "ut", "acc"]

    onesf = sb("onesf", [128, 128]); epst = sb("epst", [128, 1]); onec = sb("onec", [128, 1])
    A = pg.op
    cnt = {"x": 0, "A": 0, "S": 0, "p": 0, "y": 0, "o": 0, "xs": 0, "w": 0, "kb": 0, "st": 0, "B": 0, "ph": 0}

    def nxt(k, n):
        v = cnt[k] % n
        cnt[k] += 1
        return v

    for (t, src, nm) in ((gct, gcol, "gct"), (fgt, fg, "fgt"), (bft, bfi, "bft"), (cwt, cwi, "cwt"),
                         (cstt, cst, "cstt"), (mskt, mski, "mskt"), (w4t, w4i, "w4t"), (mrt, mrow, "mrt")):
        A("sp", lambda e, t=t, src=src: e.dma_start(out=t[:], in_=src), writes=[nm], dma=True)
    A("dve", lambda e: e.tensor_scalar(out=cwt[:], in0=cwt[:], scalar1=0.5, scalar2=None, op0=ALU.mult), reads=["cwt"], writes=["cwt"])
    A("dve", lambda e: e.tensor_copy(out=identb[:], in_=cstt[:, 0:128]), reads=["cstt"], writes=["identb"])
    A("dve", lambda e: e.tensor_copy(out=trib[:], in_=cstt[:, 256:384]), reads=["cstt"], writes=["trib"])
    A("dve", lambda e: e.tensor_copy(out=maskb[:], in_=mskt[:].rearrange("p (a b) -> p a b", a=4)),
      reads=["mskt"], writes=["maskb"])
    A("pool", lambda e: e.memset(woab[64:128], 0.0), writes=["woab"])
    A("pool", lambda e: e.memset(onesb[:], 1.0), writes=["onesb"])
    A("pool", lambda e: e.memset(onesf[:], 1.0), writes=["onesf"])
    A("pool", lambda e: e.memset(epst[:], EPS), writes=["epst"])
    A("pool", lambda e: e.memset(onec[:], 1.0), writes=["onec"])
    for i in range(2):
        A("pool", lambda e, i=i: e.memset(vsb[i][:], 2.0), writes=[f"vsb{i}"])
        for j in range(2):
            A("pool", lambda e, i=i, j=j: e.memset(kbuf[i][j][:], 1.0), writes=[f"kb{i}{j}"])
    A("pool", lambda e: e.memset(vmt[:], 2.0), writes=["vmt"])
    A("pool", lambda e: e.memset(ssq[:], 0.0), writes=["ssq0", "ssq1", "ssq2", "ssq3"])

    wq = []
    wstate = {"issued": 0, "done": 0}

    def load_w(dst, src, ncols, nm, chunks, scaled, np_=128):
        for c in range(chunks):
            for x0 in range(0, ncols, 1024):
                x1 = min(ncols, x0 + 1024) if ncols != 1032 else 1032
                wq.append(dict(dst=dst, src=src, c=c, x0=x0, x1=x1, nm=nm, scaled=scaled, np_=np_))
                if ncols == 1032:
                    break

    def _w_dma(i):
        w_ = wq[i]
        k = i % 2
        A("act", lambda e: e.dma_start(out=wst[k][0:w_["np_"], 0:w_["x1"] - w_["x0"]], in_=w_["src"][:, w_["c"], w_["x0"]:w_["x1"]]),
          writes=[f"wst{k}"], dma=True)

    def emit_w(n):
        for _ in range(n):
            if wstate["done"] >= len(wq):
                return
            while wstate["issued"] < len(wq) and wstate["issued"] <= wstate["done"] + 1:
                _w_dma(wstate["issued"])
                wstate["issued"] += 1
            i = wstate["done"]
            w_ = wq[i]
            k = i % 2
            if w_["scaled"]:
                A("act", lambda e, w_=w_, k=k: e.activation(out=w_["dst"][0:w_["np_"], w_["c"], w_["x0"]:w_["x1"]],
                                                           in_=wst[k][0:w_["np_"], 0:w_["x1"] - w_["x0"]], func=AF.Copy,
                                                           scale=gct[0:w_["np_"], w_["c"]:w_["c"] + 1]),
                  reads=[f"wst{k}", "gct"], writes=[w_["nm"]])
            else:
                A("act", lambda e, w_=w_, k=k: e.activation(out=w_["dst"][0:w_["np_"], w_["c"], w_["x0"]:w_["x1"]],
                                                           in_=wst[k][0:w_["np_"], 0:w_["x1"] - w_["x0"]], func=AF.Copy),
                  reads=[f"wst{k}"], writes=[w_["nm"]])
            wstate["done"] += 1
            if wstate["issued"] < len(wq) and wstate["issued"] <= wstate["done"] + 1:
                _w_dma(wstate["issued"])
                wstate["issued"] += 1

    pg.mark("setup")
    load_w(W1b, w1, 1032, "W1b", KC, True)
    _w_dma(0)
    _w_dma(1)
    wstate["issued"] = 2
    pg.mark("w1")

    def norm_pre(xtile, xname, nt):
        k = nxt("xs", 2)
        q = nxt("st", 4)
        A("act", lambda e: e.activation(out=xsb[k][0:nt, :], in_=xtile[0:nt, :], func=AF.Square, accum_out=ssq[0:nt, q:q + 1]),
          reads=[xname, f"ssq{q}"], writes=[f"xsb{k}", f"ssq{q}"])
        A("act", lambda e: e.activation(out=rst[0:nt, 2 * q:2 * q + 1], in_=ssq[0:nt, q:q + 1], func=AF.Ln, scale=1.0 / D, bias=epst[0:nt, :]),
          reads=[f"ssq{q}", "epst"], writes=[f"rst{q}"])
        A("act", lambda e: e.activation(out=rst[0:nt, 2 * q + 1:2 * q + 2], in_=rst[0:nt, 2 * q:2 * q + 1], func=AF.Exp, scale=-0.5),
          reads=[f"rst{q}"], writes=[f"rst{q}"])
        A("pool", lambda e: e.memset(ssq[:, q:q + 1], 0.0), reads=[f"ssq{q}"], writes=[f"ssq{q}"])
        A("dve", lambda e: e.tensor_scalar(out=xsb[k][0:nt, :], in0=xtile[0:nt, :], scalar1=rst[0:nt, 2 * q + 1:2 * q + 2], scalar2=None,
                                           op0=ALU.mult),
          reads=[xname, f"rst{q}"], writes=[f"xsb{k}"])
        return k

    def norm_post(k, nt, dst, dname, evac_eng="act"):
        for c in range(KC):
            A("pe", lambda e, c=c: e.transpose(out=psT[:, c, 0:nt], in_=xsb[k][0:nt, c * 128:(c + 1) * 128],
                                               identity=identb[0:nt, 0:nt]),
              reads=[f"xsb{k}", "identb"], writes=["psT"])
        if evac_eng == "act":
            A("act", lambda e: e.activation(out=dst, in_=psT[:, :, 0:nt], func=AF.Copy), reads=["psT"], writes=[dname])
        else:
            A("dve", lambda e: e.tensor_copy(out=dst, in_=psT[:, :, 0:nt]), reads=["psT"], writes=[dname])

    def norm_T(xtile, xname, nt, dst, dname, evac_eng="act"):
        k = norm_pre(xtile, xname, nt)
        norm_post(k, nt, dst, dname, evac_eng)

    def p1_pre(m_, jj_):
        k = nxt("x", 3)
        blk = 4 * m_ + jj_
        A("sp", lambda e: e.dma_start(out=xt[k][:], in_=xa[blk * 128:(blk + 1) * 128, :]), writes=[f"xt{k}"], dma=True)
        return norm_pre(xt[k], f"xt{k}", 128)

    def p1_post(m_, jj_, kx):
        norm_post(kx, 128, hT[m_ % 2][:, :, jj_ * 128:(jj_ + 1) * 128], f"hT{m_ % 2}.{jj_}", evac_eng="dve")

    for jj in range(4):
        p1_post(0, jj, p1_pre(0, jj))
    emit_w(len(wq))

    def mm(ps, psn, lhs_fn, rhs_fn, rnames):
        for c in range(KC):
            A("pe", lambda e, c=c: e.matmul(ps, lhsT=lhs_fn(c), rhs=rhs_fn(c), start=(c == 0), stop=(c == KC - 1)),
              reads=list(rnames), writes=[psn])

    def final_norm_store(src_ps_pair, xres, xname, nt, dst_dram):
        for hf in range(2):
            A("dve", lambda e, hf=hf: e.tensor_tensor(out=rr[0:nt, hf * 512:(hf + 1) * 512], in0=src_ps_pair[hf][0][0:nt, :],
                                                      in1=xres[0:nt, hf * 512:(hf + 1) * 512], op=ALU.add),
              reads=[src_ps_pair[hf][1], xname], writes=[f"rr{hf}"])
        q = nxt("st", 4)
        k = nxt("xs", 2)
        A("act", lambda e: e.activation(out=xsb[k][0:nt, :], in_=rr[0:nt, :], func=AF.Square, accum_out=ssq[0:nt, q:q + 1]),
          reads=["rr0", "rr1", f"ssq{q}"], writes=[f"xsb{k}", f"ssq{q}"])
        A("act", lambda e: e.activation(out=rst[0:nt, 2 * q:2 * q + 1], in_=ssq[0:nt, q:q + 1], func=AF.Ln, scale=1.0 / D, bias=epst[0:nt, :]),
          reads=[f"ssq{q}", "epst"], writes=[f"rst{q}"])
        A("act", lambda e: e.activation(out=rst[0:nt, 2 * q + 1:2 * q + 2], in_=rst[0:nt, 2 * q:2 * q + 1], func=AF.Exp, scale=-0.5),
          reads=[f"rst{q}"], writes=[f"rst{q}"])
        A("pool", lambda e: e.memset(ssq[:, q:q + 1], 0.0), reads=[f"ssq{q}"], writes=[f"ssq{q}"])
        A("dve", lambda e: e.scalar_tensor_tensor(out=xres[0:nt, :], in0=rr[0:nt, :], scalar=rst[0:nt, 2 * q + 1:2 * q + 2], in1=fgt[0:nt, :],
                                                  op0=ALU.mult, op1=ALU.mult),
          reads=["rr0", "rr1", f"rst{q}", "fgt", xname], writes=[xname])
        A("pool", lambda e: e.dma_start(out=dst_dram, in_=xres[0:nt, :]), reads=[xname], dma=True)

    A("sp", lambda e: e.dma_start(out=xt[0][0:16, :], in_=meta), writes=["xt0"], dma=True)
    norm_T(xt[0], "xt0", 16, hTx[:, :, 0:16], "hTx")
    pg.mark("m1")
    for p in range(4):
        a = nxt("A", 2)
        mm(psA[a][:, 0:16], f"psA{a}", lambda c, p=p: W1b[:, c, p * 128:(p + 1) * 128], lambda c: hTx[:, c, 0:16], ["W1b", "hTx"])
        A("dve", lambda e, a=a, p=p: e.tensor_copy(out=ktsb[0][:, p, 0:16], in_=psA[a][:, 0:16]), reads=[f"psA{a}"], writes=["ktsb0"])
    A("pool", lambda e: e.dma_start(out=kT_scr[:, :, 0:16].rearrange("p r t -> r p t"), in_=ktsb[0][:, :, 0:16]),
      reads=["ktsb0"], writes=["kT_scr"], dma=True)
    pg.mark("m2")
    mm(psS[0][0:16, :], "psS0", lambda c: hTx[:, c, 0:16], lambda c: W1b[:, c, 512:1024], ["W1b", "hTx"])
    pg.mark("m2x")
    A("dve", lambda e: e.tensor_copy(out=vmt[0:16, :, 0, :, 0:64], in_=psS[0][0:16, :].rearrange("k (p h d) -> k p h d", p=4, h=2)),
      reads=["psS0"], writes=["vmt"])
    pg.mark("m2a")
    A("act", lambda e: e.activation(out=o32[0][0:16, :], in_=psS[0][0:16, :], func=AF.Copy), reads=["psS0"], writes=["o320"])
    A("pool", lambda e: e.dma_start(out=nv_m, in_=o32[0][0:16, :]), reads=["o320"], dma=True)
    pg.mark("m2b")
    A("pool", lambda e: e.dma_start(out=v_scr[:, :, 0:1, :].rearrange("p k b c -> k p b c"),
                                    in_=vmt[:].rearrange("k p b h c -> k p b (h c)")),
      reads=["vmt"], writes=["v_scr"], dma=True)
    pg.mark("m3")
    mm(psS[1][0:16, :], "psS1", lambda c: hTx[:, c, 0:16], lambda c: W1b[:, c, 0:512], ["W1b", "hTx"])
    A("act", lambda e: e.activation(out=o32[0][0:16, :], in_=psS[1][0:16, :], func=AF.Copy), reads=["psS1"], writes=["o320"])
    A("pool", lambda e: e.dma_start(out=nk_m, in_=o32[0][0:16, :]), reads=["o320"], dma=True)
    A("pool", lambda e: e.memset(Z[:, 0, :], 0.0), writes=["Z"])
    mm(psS[2][0:16, 0:8], "psS2", lambda c: hTx[:, c, 0:16], lambda c: W1b[:, c, 1024:1032], ["W1b", "hTx"])
    A("dve", lambda e: e.tensor_tensor(out=Z[0:16, 0, :], in0=psS[2][0:16, 0:8], in1=bft[0:16, 0, :], op=ALU.add),
      reads=["psS2", "bft"], writes=["Z"])

    pg.mark("meta")
    w2_loaded = False
    fut = {"pre": 0, "post": 0, "k": {}}
    NFUT = 4 * (NG - 1)

    def fut_pre(upto):
        while fut["pre"] < NFUT and fut["pre"] <= upto:
            i = fut["pre"]
            fut["k"][i] = p1_pre(1 + i // 4, i % 4)
            fut["pre"] += 1

    def fut_post(i):
        if i < NFUT:
            p1_post(1 + i // 4, i % 4, fut["k"].pop(i))
            fut["post"] += 1

    load_w(W2b, w2, 3072, "W2b", KC, True)
    load_w(woab, woa, D, "woab", H, False, np_=64)
    load_w(wocb, woc, D, "wocb", 4, False)
    for m in range(NG):
        par = m % 2
        hnames = [f"hT{par}.{j}" for j in range(4)]
        def v_block(jj, m=m, par=par):
                s_ = nxt("S", 2)
                mm(psS[s_][:, :], f"psS{s_}", lambda c, jj=jj: hT[par][:, c, jj * 128:(jj + 1) * 128],
                   lambda c: W1b[:, c, 512:1024], ["W1b", f"hT{par}.{jj}"])
                A("dve", lambda e, s_=s_, jj=jj: e.tensor_copy(out=vsb[par][:, :, jj, :, 0:64],
                                                              in_=psS[s_][:, :].rearrange("k (p h d) -> k p h d", p=4, h=2)),
                  reads=[f"psS{s_}"], writes=[f"vsb{par}"])
                if jj == 3:
                    ot, on = o3[nxt("o", 3)]
                    A("act", lambda e, s_=s_: e.activation(out=ot, in_=psS[s_][:, :], func=AF.Copy),
                      reads=[f"psS{s_}"], writes=[on])
                    A("pool", lambda e, m=m: e.dma_start(out=nv_own[m * 128:(m + 1) * 128, :], in_=ot),
                      reads=[on], dma=True)
                    s2 = nxt("S", 2)
                    mm(psS[s2][:, :], f"psS{s2}", lambda c: hT[par][:, c, 384:512], lambda c: W1b[:, c, 0:512],
                       ["W1b", f"hT{par}.3"])
                    ot2, on2 = o3[nxt("o", 3)]
                    A("act", lambda e, s2=s2: e.activation(out=ot2, in_=psS[s2][:, :], func=AF.Copy),
                      reads=[f"psS{s2}"], writes=[on2])
                    A("pool", lambda e, m=m: e.dma_start(out=nk_own[m * 128:(m + 1) * 128, :], in_=ot2),
                      reads=[on2], dma=True)

        for p in range(4):
            fut_pre(4 * m + p + 1)
            a = nxt("A", 2)
            mm(psA[a][:, :], f"psA{a}", lambda c, p=p: W1b[:, c, p * 128:(p + 1) * 128], lambda c: hT[par][:, c, :],
               ["W1b"] + hnames)
            A("dve", lambda e, a=a, p=p: e.tensor_copy(out=ktsb[par][:, p, :], in_=psA[a][:, :]),
              reads=[f"psA{a}"], writes=[f"ktsb{par}"])
            v_block(p)
            fut_post(4 * m + p)
            if m >= 1 or p >= 1:
                emit_w(max(1, -(-36 // (4 * NG - 4))))
        A("pool", lambda e, m=m: e.dma_start(out=kT_scr[:, :, 16 + 512 * m:16 + 512 * (m + 1)].rearrange("p r t -> r p t"),
                                             in_=ktsb[par][:]),
          reads=[f"ktsb{par}"], writes=["kT_scr"], dma=True)
        A("pool", lambda e, m=m: e.dma_start(out=v_scr[:, :, 1 + 4 * m:5 + 4 * m, :].rearrange("p k b c -> k p b c"),
                                             in_=vsb[par][:].rearrange("k p b h c -> k p b (h c)")),
          reads=[f"vsb{par}"], writes=["v_scr"], dma=True)
        for jj in range(4):
            mm(psS[2][:, jj * 8:(jj + 1) * 8], "psS2", lambda c, jj=jj: hT[par][:, c, jj * 128:(jj + 1) * 128],
               lambda c: W1b[:, c, 1024:1032], ["W1b", f"hT{par}.{jj}"])
        A("dve", lambda e, m=m: e.tensor_tensor(out=Z[:, 1 + 4 * m:5 + 4 * m, :],
                                                in0=psS[2][:, 0:32].rearrange("k (j h) -> k j h", j=4), in1=bft[:], op=ALU.add),
          reads=["psS2", "bft"], writes=["Z"])

    emit_w(len(wq))
    pg.mark("groups")
    sbk = {}

    def sbn_pre(s_, jq):
        blk = 4 * (4 * s_ + jq) + 3
        k = nxt("x", 3)
        A("sp", lambda e: e.dma_start(out=xt[k][:], in_=xa[blk * 128:(blk + 1) * 128, :]), writes=[f"xt{k}"], dma=True)
        return norm_pre(xt[k], f"xt{k}", 128)

    def sbn_post(s_, jq, kx):
        norm_post(kx, 128, hT[0][:, :, jq * 128:(jq + 1) * 128], f"hT0.{jq}", evac_eng="dve")

    for jq in range(4):
        sbn_post(0, jq, sbn_pre(0, jq))
    n_all = NBLK1 * H
    Zf = Z[:].rearrange("p b h -> p (b h)"); LFf = LF[:].rearrange("p b h -> p (b h)")
    CLf = CL[:].rearrange("p b h -> p (b h)"); TOTf = TOT[:].rearrange("p b h -> p (b h)")
    A("act", lambda e: e.activation(out=LFf, in_=Zf, func=AF.Exp, scale=-1.0), reads=["Z"], writes=["LF"])
    A("act", lambda e: e.activation(out=LFf, in_=LFf, func=AF.Ln, bias=onec[:, :]), reads=["LF", "onec"], writes=["LF"])
    A("dve", lambda e: e.tensor_scalar(out=LFf, in0=LFf, scalar1=-1.0, scalar2=None, op0=ALU.mult), reads=["LF"], writes=["LF"])
    A("dve", lambda e: e.tensor_scalar(out=LF[:, 0, :], in0=LF[:, 0, :], scalar1=mrt[:, 0:1], scalar2=None, op0=ALU.mult),
      reads=["LF", "mrt"], writes=["LF"])
    A("pool", lambda e: e.dma_start(out=nlf_own.rearrange("(m t) h -> t m h", t=128),
                                    in_=LF[:, 1:NBLK1, :].rearrange("p (m j) h -> p m j h", j=4)[:, :, 3, :]),
      reads=["LF"], dma=True)
    A("pool", lambda e: e.dma_start(out=nlf_m, in_=LF[0:16, 0, :]), reads=["LF"], dma=True)
    for c0 in range(0, n_all, 512):
        c1 = min(n_all, c0 + 512)
        A("pe", lambda e, c0=c0, c1=c1: e.matmul(psO[0][:, 0:c1 - c0], lhsT=cstt[:, 128:256], rhs=LFf[:, c0:c1], start=True, stop=True),
          reads=["cstt", "LF"], writes=["psO0"])
        A("dve", lambda e, c0=c0, c1=c1: e.tensor_copy(out=CLf[:, c0:c1], in_=psO[0][:, 0:c1 - c0]), reads=["psO0"], writes=["CL"])
        A("pe", lambda e, c0=c0, c1=c1: e.matmul(psO[1][:, 0:c1 - c0], lhsT=onesf[:], rhs=LFf[:, c0:c1], start=True, stop=True),
          reads=["onesf", "LF"], writes=["psO1"])
        A("dve", lambda e, c0=c0, c1=c1: e.tensor_copy(out=TOTf[:, c0:c1], in_=psO[1][:, 0:c1 - c0]), reads=["psO1"], writes=["TOT"])
    T4 = TOT[:, 1:NBLK1, :].rearrange("p (m j) h -> p m j h", j=4)
    BP4 = BP[:, 1:NBLK1, :].rearrange("p (m j) h -> p m j h", j=4)
    CL4 = CL[:, 1:NBLK1, :].rearrange("p (m j) h -> p m j h", j=4)
    A("dve", lambda e: e.tensor_copy(out=sa[0][:, 0, :], in_=TOT[:, 0, :]), reads=["TOT"], writes=["sa0"])
    A("dve", lambda e: e.tensor_tensor(out=sa[0][:, 1:NG + 1, :], in0=T4[:, :, 0, :], in1=T4[:, :, 1, :], op=ALU.add),
      reads=["TOT", "sa0"], writes=["sa0"])
    for j in (2, 3):
        A("dve", lambda e, j=j: e.tensor_tensor(out=sa[0][:, 1:NG + 1, :], in0=sa[0][:, 1:NG + 1, :], in1=T4[:, :, j, :], op=ALU.add),
          reads=["TOT", "sa0"], writes=["sa0"])
    cur = 0
    d = 1
    while d < NG + 1:
        A("dve", lambda e, cur=cur, d=d: e.tensor_tensor(out=sa[1 - cur][:, d:NG + 1, :], in0=sa[cur][:, d:NG + 1, :],
                                                         in1=sa[cur][:, 0:NG + 1 - d, :], op=ALU.add),
          reads=[f"sa{cur}"], writes=[f"sa{1 - cur}"])
        A("dve", lambda e, cur=cur, d=d: e.tensor_copy(out=sa[1 - cur][:, 0:d, :], in_=sa[cur][:, 0:d, :]),
          reads=[f"sa{cur}", f"sa{1 - cur}"], writes=[f"sa{1 - cur}"])
        cur = 1 - cur
        d *= 2
    A("pool", lambda e: e.memset(BP[:, 0, :], 0.0), writes=["BP"])
    for jj in range(4):
        A("dve", lambda e, jj=jj, cur=cur: e.tensor_copy(out=BP4[:, :, jj, :], in_=sa[cur][:, 0:NG, :]),
          reads=[f"sa{cur}", "BP"], writes=["BP"])
        for j2 in range(4):
            if j2 == jj:
                continue
            A("dve", lambda e, jj=jj, j2=j2: e.scalar_tensor_tensor(out=BP4[:, :, jj, :], in0=T4[:, :, j2, :],
                                                                    scalar=w4t[:, j2 * 4 + jj:j2 * 4 + jj + 1],
                                                                    in1=BP4[:, :, jj, :], op0=ALU.mult, op1=ALU.add),
              reads=["TOT", "w4t", "BP"], writes=["BP"])
    A("dve", lambda e: e.tensor_tensor(out=CL[:], in0=CL[:], in1=BP[:], op=ALU.add), reads=["CL", "BP"], writes=["CL"])
    A("dve", lambda e: e.tensor_scalar(out=NC[:], in0=CL[:], scalar1=-1.0, scalar2=None, op0=ALU.mult), reads=["CL"], writes=["NC"])

    def split3(src8, dstCP, nm_src, nm_dst, tmp, nm_tmp):
        A("dve", lambda e: e.tensor_copy(out=dstCP[:, :, :, 0], in_=src8), reads=[nm_src], writes=[nm_dst])
        A("dve", lambda e: e.tensor_tensor(out=tmp, in0=src8, in1=dstCP[:, :, :, 0], op=ALU.subtract), reads=[nm_src, nm_dst], writes=[nm_tmp])
        A("dve", lambda e: e.tensor_copy(out=dstCP[:, :, :, 1], in_=tmp), reads=[nm_tmp, nm_dst], writes=[nm_dst])
        A("dve", lambda e: e.tensor_tensor(out=tmp, in0=tmp, in1=dstCP[:, :, :, 1], op=ALU.subtract), reads=[nm_tmp, nm_dst], writes=[nm_tmp])
        A("dve", lambda e: e.tensor_copy(out=dstCP[:, :, :, 2], in_=tmp), reads=[nm_tmp, nm_dst], writes=[nm_dst])

    A("dve", lambda e: e.tensor_scalar(out=c8[:], in0=CL4[:, :, 3, :], scalar1=8.0, scalar2=None, op0=ALU.mult), reads=["CL"], writes=["c8"])
    split3(c8[:], CPt[:], "c8", "CPt", r1[:], "r1")
    ck8 = wst[0][:, 0:nb8].rearrange("p (b h) -> p b h", h=H)
    ckt = wst[1][:, 0:nb8].rearrange("p (b h) -> p b h", h=H)
    CPk = ktsb[0].rearrange("p a b -> p (a b)")[:, 0:nb8 * 3].rearrange("p (b h r) -> p b h r", h=H, r=3)
    A("dve", lambda e: e.tensor_scalar(out=ck8, in0=NC[:], scalar1=8.0, scalar2=None, op0=ALU.mult), reads=["NC", "wst0"], writes=["wst0"])
    split3(ck8, CPk, "wst0", "ktsb0", ckt, "wst1")
    a = nxt("A", 2)
    A("pe", lambda e: e.matmul(psA[a][0:24, 0:16], lhsT=CPk[0:16, 0, :, :].rearrange("p h r -> p (h r)"), rhs=identb[0:16, 0:16],
                               start=True, stop=True), reads=["ktsb0", "identb"], writes=[f"psA{a}"])
    A("dve", lambda e: e.tensor_copy(out=ktsb[1][0:24, 0, 0:16], in_=psA[a][0:24, 0:16]), reads=[f"psA{a}"], writes=["ktsb1"])
    A("pool", lambda e: e.dma_start(out=kaug_scr[:, 0:16], in_=ktsb[1][0:24, 0, 0:16]), reads=["ktsb1"], writes=["kaug_scr"], dma=True)
    _b5 = [(psA[0], "psA0"), (psA[1], "psA1"), (psS[0], "psS0"), (psS[1], "psS1"), (psO[0], "psO0"), (psO[1], "psO1")]
    for g in range(NG):
        pa, na = _b5[g % 6]
        for j in range(4):
            A("pe", lambda e, j=j: e.matmul(pa[0:24, j * 128:(j + 1) * 128],
                                            lhsT=CPk[:, 1 + 4 * g + j, :, :].rearrange("p h r -> p (h r)"), rhs=identb[:],
                                            start=True, stop=True), reads=["ktsb0", "identb"], writes=[na])
        kk = 1 + g % 3
        A("dve", lambda e: e.tensor_copy(out=ktsb[1][0:24, kk, :], in_=pa[0:24, :]), reads=[na], writes=[f"ktsb1.{kk}"])
        A("pool", lambda e: e.dma_start(out=kaug_scr[:, 16 + 512 * g:16 + 512 * (g + 1)], in_=ktsb[1][0:24, kk, :]),
          reads=[f"ktsb1.{kk}"], writes=["kaug_scr"], dma=True)

    pg.mark("csum")
    fence_t = sb("fence_t", [128, 1])
    A("pool", lambda e: e.memset(fence_t[:], 0.0), reads=[], writes=P1_NAMES + P2_NAMES + ["hT1.0", "hT1.1", "hT1.2", "hT1.3", "pT2"])
    qa_e = qa.rearrange("p (a two) c -> p a two c", two=2)
    A("pool", lambda e: e.memset(qa_e[64:128, :, 0, :], 0.0), writes=["qa"])
    A("pool", lambda e: e.memset(qa_e[64:70, :, 0, :], 1.0), reads=["qa"], writes=["qa"])
    A("pool", lambda e: e.memset(qa_e[0:64, :, 1, :], 0.0), reads=["qa"], writes=["qa"])
    A("pool", lambda e: e.memset(qa_e[0:6, :, 1, :], 1.0), reads=["qa"], writes=["qa"])
    A("pool", lambda e: e.memset(ca[64:128], 0.0), writes=["ca"])
    _b7 = [(psA[0], "psA0"), (psA[1], "psA1"), (psS[0], "psS0"), (psS[1], "psS1"), (psS[2], "psS2"), (psO[0], "psO0"), (psO[1], "psO1")]

    def bank7():
        return _b7[nxt("B", 7)]

    def silu_gate(ps, psn, npart, ncol, out_ap, out_name, other=None, other_name=None):
        A("act", lambda e: e.activation(out=t5[1][0:npart, 0:ncol], in_=ps[0:npart, 0:ncol], func=AF.Tanh, scale=0.5),
          reads=[psn], writes=["t51", "t51b", "t51c"])
        if other is None:
            A("dve", lambda e: e.scalar_tensor_tensor(out=out_ap, in0=t5[1][0:npart, 0:ncol], scalar=1.0, in1=ps[0:npart, 0:ncol],
                                                      op0=ALU.add, op1=ALU.mult),
              reads=[psn, "t51"], writes=[out_name])
        else:
            A("dve", lambda e: e.scalar_tensor_tensor(out=t5[2][0:npart, 0:ncol], in0=t5[1][0:npart, 0:ncol], scalar=1.0,
                                                      in1=ps[0:npart, 0:ncol], op0=ALU.add, op1=ALU.mult),
              reads=[psn, "t51"], writes=["t52"])
            A("dve", lambda e: e.tensor_tensor(out=out_ap, in0=other, in1=t5[2][0:npart, 0:ncol], op=ALU.mult),
              reads=["t52", other_name], writes=[out_name])

    def conv_chunk(j, hsrc, hnames, ncol, u_view, u_in, halo_fn, acc_v, out_ap):
        pa, na = bank7()
        mm(pa[:, 0:ncol], na, lambda c: W2b[:, c, 2048 + 128 * j:2048 + 128 * (j + 1)], hsrc, ["W2b"] + hnames)
        A("act", lambda e: e.activation(out=t5[0][:, 0:ncol], in_=pa[:, 0:ncol], func=AF.Copy), reads=[na], writes=["t50"])
        pb, nb_ = bank7()
        mm(pb[:, 0:ncol], nb_, lambda c: W2b[:, c, 1536 + 128 * j:1536 + 128 * (j + 1)], hsrc, ["W2b"] + hnames)
        nb = u_in.shape[1]
        L = u_in.shape[2]
        A("dve", lambda e: e.tensor_tensor(out=u_in, in0=pb[:, 0:ncol].rearrange("p (b l) -> p b l", b=nb),
                                           in1=t5[0][:, 0:ncol].rearrange("p (b l) -> p b l", b=nb), op=ALU.mult),
          reads=[nb_, "t50"], writes=["ut"])
        halo_fn()
        A("dve", lambda e: e.tensor_scalar(out=acc_v, in0=u_view[:, :, 2:2 + L], scalar1=cwt[:, j, 2:3], scalar2=None, op0=ALU.mult),
          reads=["ut", "cwt"], writes=["acc"])
        for i in (1, 0):
            A("dve", lambda e, i=i: e.scalar_tensor_tensor(out=acc_v, in0=u_view[:, :, i:i + L], scalar=cwt[:, j, i:i + 1], in1=acc_v,
                                                           op0=ALU.mult, op1=ALU.add),
              reads=["ut", "cwt", "acc"], writes=["acc"])
        pc, nc_ = bank7()
        mm(pc[:, 0:ncol], nc_, lambda c: W2b[:, c, 1024 + 128 * j:1024 + 128 * (j + 1)], hsrc, ["W2b"] + hnames)
        A("dve", lambda e: e.tensor_tensor(out=acc[:, 0:ncol], in0=acc[:, 0:ncol], in1=pc[:, 0:ncol], op=ALU.mult),
          reads=[nc_, "acc"], writes=["acc"])
        pd, nd = bank7()
        mm(pd[:, 0:ncol], nd, lambda c: W2b[:, c, 2560 + 128 * j:2560 + 128 * (j + 1)], hsrc, ["W2b"] + hnames)
        silu_gate(pd, nd, 128, ncol, out_ap, "catc", other=acc[:, 0:ncol], other_name="acc")

    def u_last2(hcols, dst_dram):
        mm(psS[0][0:2, :], "psS0", hcols, lambda c: W2b[:, c, 1536:2048], ["W2b", "hTx"])
        mm(psS[1][0:2, :], "psS1", hcols, lambda c: W2b[:, c, 2048:2560], ["W2b", "hTx"])
        A("act", lambda e: e.activation(out=t5[0][0:2, :], in_=psS[1][0:2, :], func=AF.Copy), reads=["psS1"], writes=["t50"])
        A("dve", lambda e: e.tensor_tensor(out=t5[2][0:2, :], in0=psS[0][0:2, :], in1=t5[0][0:2, :], op=ALU.mult), reads=["psS0", "t50"], writes=["t52"])
        A("pool", lambda e: e.dma_start(out=dst_dram, in_=t5[2][0:2, :]), reads=["t52"], dma=True)

    A("sp", lambda e: e.dma_start(out=xt[0][0:NH, :], in_=xh), writes=["xt0"], dma=True)
    norm_T(xt[0], "xt0", NH, hTx[:, :, 0:NH], "hTx")
    for j in range(4):
        a = nxt("A", 2)
        mm(psA[a][:, 0:NH], f"psA{a}", lambda c, j=j: W2b[:, c, 2048 + 128 * j:2048 + 128 * (j + 1)], lambda c: hTx[:, c, 0:NH], ["W2b", "hTx"])
        A("act", lambda e, a=a: e.activation(out=t5[0][:, 0:NH], in_=psA[a][:, 0:NH], func=AF.Copy), reads=[f"psA{a}"], writes=["t50"])
        b = nxt("A", 2)
        mm(psA[b][:, 0:NH], f"psA{b}", lambda c, j=j: W2b[:, c, 1536 + 128 * j:1536 + 128 * (j + 1)], lambda c: hTx[:, c, 0:NH], ["W2b", "hTx"])
        A("dve", lambda e, b=b, j=j: e.tensor_tensor(out=uh[:, j, :], in0=psA[b][:, 0:NH], in1=t5[0][:, 0:NH], op=ALU.mult),
          reads=[f"psA{b}", "t50"], writes=["uh"])
    u_last2(lambda c: hTx[:, c, NH - 2:NH], ncv)

    pg.mark("halo")

    smode = {"on": False}
    pend = []
    hold = []

    def _pairable(a, b, na, nb_):
        return (a["bias"] is None and b["bias"] is None and a["nk"] == 128 and b["nk"] == 128 and a["q0"] == b["q0"] and na == nb_)

    def attn_push(it, qa_t, ncol):
        if hold:
            (a, qa_a, na) = hold.pop()
            if _pairable(a, it, na, ncol) and qa_a is qa_t:
                _unit([a, it], qa_t, ncol)
                return
            _unit([a], qa_a, na)
        if it["bias"] is None and it["nk"] == 128:
            hold.append((it, qa_t, ncol))
        else:
            _unit([it], qa_t, ncol)

    def _unit(items, qa_t, ncol):
        P_ = nxt("S", 2)
        for i, it in enumerate(items):
            nk, q0 = it["nk"], it["q0"]
            ps = psSS[P_][:, i * 512:(i + 1) * 512]
            A("pe", lambda e, it=it, ps=ps, nk=nk, q0=q0: e.matmul(ps[0:nk, q0:ncol], lhsT=it["klhs"], rhs=qa_t[0:128, it["h"], q0:ncol],
                                                                  start=True, stop=(it["mask"] is None)),
              reads=[it["knm"], "qa"], writes=[SNAMES[P_][i]])
            if it["mask"] is not None:
                mk, w = it["mask"]
                A("pe", lambda e, ps=ps, nk=nk, q0=q0, mk=mk, w=w: e.matmul(ps[0:nk, q0:q0 + w], lhsT=identb[0:nk, 0:nk], rhs=mk,
                                                                           start=False, stop=True),
                  reads=["identb", "maskb", "trib"], writes=[SNAMES[P_][i]])
        pend.append((items, P_, ncol))
        if len(pend) > 1:
            attn_pop()

    def attn_pop():
        items, P_, ncol = pend.pop(0)
        n = len(items)
        nk, q0 = items[0]["nk"], items[0]["q0"]
        src = psSS[P_][0:nk, :].rearrange("p (a b) -> p a b", a=2)[:, 0:n, q0:ncol]
        rd = [SNAMES[P_][i] for i in range(n)]
        if smode["on"]:
            Q_ = 0
            hh = nxt("ph", 2)
            pdst = pTT[0][:, hh:hh + 1, :]
            pname = "pT0a" if hh == 0 else "pT0b"
        else:
            Q_ = nxt("p", 2)
            pdst = pTT[Q_]
            pname = PNAMES[Q_]
        dst = pdst[0:nk, 0:n, q0:ncol]
        if items[0]["bias"] is None:
            A("act", lambda e: e.activation(out=dst, in_=src, func=AF.Exp, scale=0.125), reads=rd, writes=[pname])
        else:
            A("act", lambda e: e.activation(out=dst, in_=src, func=AF.Exp, bias=items[0]["bias"], scale=0.125),
              reads=rd + [items[0].get("bnm", "NC")], writes=[pname])
        for i, it in enumerate(items):
            hs = it["hslot"]
            A("pe", lambda e, it=it, i=i, hs=hs: e.matmul(psO[hs][0:65, q0:ncol], lhsT=it["vlhs"], rhs=pdst[0:nk, i, q0:ncol],
                                                         start=it["first"], stop=it["last"]),
              reads=[it["vnm"], pname], writes=[f"psO{hs}"])

    def attn_flush():
        while hold:
            (a, qa_a, na) = hold.pop()
            _unit([a], qa_a, na)
        while pend:
            attn_pop()

    def attn_epilogue_multi(heads, ncol, c0=0):
        rdv = {hs: t5[2][64:65, hs * 0 + c0:ncol] for hs, _ in heads}
        bufs = {}
        for i, (hs, h) in enumerate(heads):
            r32 = (t5[2] if i == 0 else acc)
            rh = (rdh_v if i == 0 else ut.rearrange("p a b -> p (a b)")[:, 0:256].bitcast(BF16))
            rl = (rdl_v if i == 0 else ut.rearrange("p a b -> p (a b)")[:, 256:512].bitcast(BF16))
            bufs[hs] = (r32, rh, rl, ("t52" if i == 0 else "acc"), ("t51" if i == 0 else "ut"), ("t51b" if i == 0 else "utb"))
        for hs, h in heads:
            r32, rh, rl, n32, nh, nl = bufs[hs]
            A("act", lambda e, r32=r32, hs=hs: e.activation(out=r32[64:65, c0:ncol], in_=psO[hs][64:65, c0:ncol], func=AF.Ln),
              reads=[f"psO{hs}"], writes=[n32])
        for hs, h in heads:
            r32, rh, rl, n32, nh, nl = bufs[hs]
            A("act", lambda e, r32=r32: e.activation(out=r32[64:65, c0:ncol], in_=r32[64:65, c0:ncol], func=AF.Exp, scale=-1.0),
              reads=[n32], writes=[n32])
        for hs, h in heads:
            r32, rh, rl, n32, nh, nl = bufs[hs]
            A("dve", lambda e, r32=r32, rh=rh: e.tensor_copy(out=rh[64:65, c0:ncol], in_=r32[64:65, c0:ncol]), reads=[n32], writes=[nh])
        for hs, h in heads:
            r32, rh, rl, n32, nh, nl = bufs[hs]
            A("dve", lambda e, r32=r32, rh=rh: e.tensor_tensor(out=r32[64:65, c0:ncol], in0=r32[64:65, c0:ncol], in1=rh[64:65, c0:ncol],
                                                              op=ALU.subtract), reads=[n32, nh], writes=[n32])
        for hs, h in heads:
            r32, rh, rl, n32, nh, nl = bufs[hs]
            A("dve", lambda e, r32=r32, rl=rl: e.tensor_copy(out=rl[64:65, c0:ncol], in_=r32[64:65, c0:ncol]), reads=[n32], writes=[nl])
        pbank = {}
        for hs, h in heads:
            r32, rh, rl, n32, nh, nl = bufs[hs]
            a = nxt("A", 2)
            pbank[hs] = a
            A("pe", lambda e, a=a, rh=rh: e.matmul(psA[a][0:64, c0:ncol], lhsT=onesb[64:65, 0:64], rhs=rh[64:65, c0:ncol], start=True, stop=False),
              reads=["onesb", nh], writes=[f"psA{a}"])
            A("pe", lambda e, a=a, rl=rl: e.matmul(psA[a][0:64, c0:ncol], lhsT=onesb[64:65, 0:64], rhs=rl[64:65, c0:ncol], start=False, stop=True),
              reads=["onesb", nl], writes=[f"psA{a}"])
        for i, (hs, h) in enumerate(heads):
            a = pbank[hs]
            rbt = t5[0] if i == 0 else t5[1]
            rbn = "t50" if i == 0 else "t51c"
            A("dve", lambda e, hs=hs, h=h, rbt=rbt: e.tensor_tensor(out=rbt[0:64, c0:ncol], in0=psO[hs][0:64, c0:ncol], in1=sz[0:64, h, c0:ncol], op=ALU.mult),
              reads=[f"psO{hs}", "sz"], writes=[rbn])
            A("dve", lambda e, h=h, rbt=rbt, a=a: e.tensor_tensor(out=ca[0:64, h, c0:ncol], in0=psA[a][0:64, c0:ncol], in1=rbt[0:64, c0:ncol], op=ALU.mult),
              reads=[rbn, f"psA{a}"], writes=["ca"])

    def attn_epilogue(hs, h, ncol, c0=0):
        attn_epilogue_multi([(hs, h)], ncol, c0)

    def out_proj(tok0, nt, xres, xname, dst):
        prs = []
        for hf in range(2):
            pa, na = bank7()
            for h in range(H):
                A("pe", lambda e, h=h, hf=hf: e.matmul(pa[0:nt, :], lhsT=ca[0:128, h, tok0:tok0 + nt],
                                                       rhs=woab[0:128, h, hf * 512:(hf + 1) * 512], start=(h == 0), stop=False),
                  reads=["ca", "woab"], writes=[na])
            for j in range(4):
                A("pe", lambda e, j=j, hf=hf: e.matmul(pa[0:nt, :], lhsT=catc[:, j, tok0:tok0 + nt],
                                                       rhs=wocb[:, j, hf * 512:(hf + 1) * 512], start=False, stop=(j == 3)),
                  reads=["catc", "wocb"], writes=[na])
            prs.append((pa, na))
        final_norm_store(prs, xres, xname, nt, dst)

    for s in range(NSB):
        hn = [f"hT0.{j}" for j in range(4)]
        hsrc = lambda c: hT[0][:, c, :]
        for p in range(4):
            pa, na = bank7()
            mm(pa[:, :], na, lambda c, p=p: W2b[:, c, p * 128:(p + 1) * 128], hsrc, ["W2b"] + hn)
            pg_, ng_ = bank7()
            for jq in range(4):
                A("pe", lambda e, p=p, jq=jq: e.matmul(pg_[64:67, jq * 128:(jq + 1) * 128], lhsT=CPt[:, 4 * s + jq, 2 * p, :],
                                                       rhs=identb[:], start=True, stop=True, tile_position=(0, 64)),
                  reads=["CPt", "identb"], writes=[ng_])
                A("pe", lambda e, p=p, jq=jq: e.matmul(pg_[0:3, jq * 128:(jq + 1) * 128], lhsT=CPt[:, 4 * s + jq, 2 * p + 1, :],
                                                       rhs=identb[:], start=True, stop=True, tile_position=(0, 0)),
                  reads=["CPt", "identb"], writes=[ng_])
            A("dve", lambda e, p=p: e.tensor_copy(out=qa[0:64, 2 * p, :], in_=pa[0:64, :]), reads=[na], writes=["qa"])
            A("dve", lambda e, p=p: e.tensor_copy(out=qa[64:128, 2 * p + 1, :], in_=pa[64:128, :]), reads=[na], writes=["qa"])
            A("dve", lambda e, p=p: e.tensor_copy(out=qa[64:67, 2 * p, :], in_=pg_[64:67, :]), reads=[ng_], writes=["qa"])
            A("dve", lambda e, p=p: e.tensor_copy(out=qa[0:3, 2 * p + 1, :], in_=pg_[0:3, :]), reads=[ng_], writes=["qa"])
        for h in range(H):
            pb, nb_ = bank7()
            mm(pb[0:64, :], nb_, lambda c, h=h: W2b[:, c, 512 + h * 64:512 + (h + 1) * 64], hsrc, ["W2b"] + hn)
            silu_gate(pb, nb_, 64, 512, sz[0:64, h, :], "sz")
        for j in range(4):
            def halo(j=j):
                A("pool", lambda e: e.tensor_copy(out=ut[:, :, 0:2], in_=uh[:, j, 8 * s:8 * s + 8].rearrange("p (b l) -> p b l", b=4)),
                  reads=["uh"], writes=["ut"])
            conv_chunk(j, hsrc, hn, 512, ut[:], ut[:, :, 2:130], halo, acc[:].rearrange("p (b l) -> p b l", b=4), catc[:, j, :])
        for p in range(4):
            NCH = 2 * (s + 1)
            for ch in range(NCH):
                slot = nxt("kb", 2)
                ncols = KCOL if ch == 0 else KCH * 128
                c0 = 0 if ch == 0 else 16 + KCH * 128 * ch
                for h2 in range(2):
                    d0, a0_ = (0, 67) if h2 == 0 else (64, 3)
                    A("sp", lambda e, slot=slot, h2=h2, p=p, c0=c0, ncols=ncols, d0=d0: e.dma_start(
                        out=kbuf[slot][h2][d0:d0 + 64, 0:ncols], in_=kT_scr[p, h2 * 64:(h2 + 1) * 64, c0:c0 + ncols]),
                      reads=["kT_scr"], writes=[f"kb{slot}{h2}"], dma=True)
                    A("sp", lambda e, slot=slot, h2=h2, p=p, c0=c0, ncols=ncols, a0_=a0_: e.dma_start(
                        out=kbuf[slot][h2][a0_:a0_ + 3, 0:ncols], in_=kaug_scr[3 * (2 * p + h2):3 * (2 * p + h2) + 3, c0:c0 + ncols]),
                      reads=["kaug_scr", f"kb{slot}{h2}"], writes=[f"kb{slot}{h2}"], dma=True)
                nvb = KCH + 1 if ch == 0 else KCH
                b0 = 0 if ch == 0 else 1 + KCH * ch
                A("sp", lambda e, slot=slot, p=p, b0=b0, nvb=nvb: e.dma_start(out=vbuf[slot][:, 0:nvb, :], in_=v_scr[p, :, b0:b0 + nvb, :]),
                  reads=["v_scr"], writes=[f"vb{slot}"], dma=True)
                items = []
                for h2 in range(2):
                    h = 2 * p + h2
                    if ch == 0:
                        items.append(dict(nk=16, klhs=kbuf[slot][h2][0:128, 0:16], knm=f"kb{slot}{h2}", q0=0, mask=None,
                                          bias=None, vlhs=vbuf[slot][0:16, 0, h2 * 66:h2 * 66 + 65], vnm=f"vb{slot}",
                                          hslot=h2, h=h, first=True, last=False))
                    for g in range(2):
                        mk_ = 2 * ch + g
                        for jj in range(4):
                            kc0 = (16 if ch == 0 else 0) + g * 512 + jj * 128
                            vb_ = (1 if ch == 0 else 0) + g * 4 + jj
                            q0 = 0 if mk_ < 4 * s else (mk_ - 4 * s) * 128
                            mask = (maskb[:, jj, :], 128) if mk_ >= 4 * s else None
                            items.append(dict(nk=128, klhs=kbuf[slot][h2][0:128, kc0:kc0 + 128], knm=f"kb{slot}{h2}", q0=q0, mask=mask,
                                              bias=None, vlhs=vbuf[slot][:, vb_, h2 * 66:h2 * 66 + 65],
                                              vnm=f"vb{slot}", hslot=h2, h=h, first=False,
                                              last=(ch == NCH - 1 and g == 1 and jj == 3)))
                for it in items:
                    attn_push(it, qa, 512)
            attn_flush()
            if s + 1 < NSB:
                if p == 0:
                    sbk[0] = sbn_pre(s + 1, 0); sbk[1] = sbn_pre(s + 1, 1)
                elif p == 1:
                    sbn_post(s + 1, 0, sbk[0]); sbn_post(s + 1, 1, sbk[1])
                    sbk[2] = sbn_pre(s + 1, 2); sbk[3] = sbn_pre(s + 1, 3)
                elif p == 2:
                    sbn_post(s + 1, 2, sbk[2]); sbn_post(s + 1, 3, sbk[3])
            attn_epilogue_multi([(0, 2 * p), (1, 2 * p + 1)], 512)
        for jq in range(4):
            mq = 4 * s + jq
            blk = 4 * mq + 3
            k = nxt("x", 3)
            A("sp", lambda e, k=k, blk=blk: e.dma_start(out=xt[k][:], in_=xa[blk * 128:(blk + 1) * 128, :]),
              writes=[f"xt{k}"], dma=True)
            out_proj(jq * 128, 128, xt[k], f"xt{k}", y_own[mq * 128:(mq + 1) * 128, :])
    pg.mark("phase2")
    if do_sample:
        NBS = PB + 1
        smode["on"] = True
        ckb = hT[1]
        vcs = sb("vcs", [128, NBS, H, 66], BF16)
        Zs = sb("Zs", [16, SBC, H]); LFs = sb("LFs", [128, SBC, NBS, H]); CLs = sb("CLs", [128, SBC, NBS, H])
        TOTs = sb("TOTs", [128, SBC, NBS, H]); BPs = sb("BPs", [128, SBC, NBS, H])
        NCs = sb("NCs", [128, SBC, NBS, H])[:] if SBC * NBS > NBLK1 else NC[:, 0:SBC * NBS, :].rearrange("p (b k) h -> p b k h", b=SBC)
        c8s = sb("c8s", [16, SBC, H]); r1s = sb("r1s", [16, SBC, H]); CPs = sb("CPs", [16, SBC, H, 3], BF16)
        sct = sb("sct", [128, 4, SBC, 2]); uts = sb("uts", [128, SBC, 18])
        A("pool", lambda e: e.memset(fence_t[:], 0.0), reads=[], writes=["hT1.0", "hT1.1", "hT1.2", "hT1.3", "pT2", "pT0", "pT0a", "pT0b"] + [f"kb{i}{j}{x}" for i in range(2) for j in range(2) for x in ("", ".aug")])
        A("sp", lambda e: e.dma_start(out=xt[0][0:NS, :], in_=xs), writes=["xt0"], dma=True)
        A("sp", lambda e: e.dma_start(out=sct[:], in_=scT), writes=["sct"], dma=True)
        norm_T(xt[0], "xt0", NS, hTx[:, :, 0:NS], "hTx")
        A("pool", lambda e: e.memset(LFs[:], 0.0), writes=["LFs"])
        A("pool", lambda e: e.memset(vcs[:], 2.0), writes=["vcs"])
        ckbs = [hT[1], hT[0]]
        cknm = [["hT1.0", "hT1.1", "hT1.2", "hT1.3"], ["hT0.0", "hT0.1", "hT0.2", "hT0.3"]]

        def load_k(b):
            for k0 in range(0, PB, 2):
                k = 1 + nxt("x", 2)
                A("sp", lambda e, b=b, k=k, k0=k0: e.dma_start(out=xt[k][:].rearrange("p (a f) -> p a f", a=2),
                                                              in_=cki[b, k0 * 128:(k0 + 2) * 128, :].rearrange("(a t) f -> t a f", t=128)),
                  writes=[f"xt{k}"], dma=True)
                A("dve", lambda e, k=k, k0=k0, b=b: e.tensor_copy(out=ckbs[b % 2][:, k0:k0 + 2, :], in_=xt[k][:].rearrange("p (a f) -> p a f", a=2)),
                  reads=[f"xt{k}"], writes=cknm[b % 2])

        def load_cache(b):
            for k0 in range(0, PB, 2):
                k = 1 + nxt("x", 2)
                A("sp", lambda e, b=b, k=k, k0=k0: e.dma_start(out=xt[k][:].rearrange("p (a f) -> p a f", a=2),
                                                              in_=cvi[b, k0 * 128:(k0 + 2) * 128, :].rearrange("(a t) f -> t a f", t=128)),
                  writes=[f"xt{k}"], dma=True)
                A("dve", lambda e, k=k, k0=k0: e.tensor_copy(out=vcs[:, k0:k0 + 2, :, 0:64],
                                                             in_=xt[k][:].rearrange("p (a h d) -> p a h d", a=2, h=H)),
                  reads=[f"xt{k}", "vcs"], writes=["vcs"])

        for i_ in range(2):
            A("pool", lambda e, i_=i_: e.memset(kbuf[i_][1][64:70, :], 1.0), reads=[f"kb{i_}1"], writes=[f"kb{i_}1", f"kb{i_}1.aug"])
        for b_ in range(SBC):
            load_k(b_)
        load_cache(0)
        for b in range(SBC):
            A("sp", lambda e, b=b: e.dma_start(out=LFs[:, b, 0:PB, :], in_=clf[b].rearrange("(k t) h -> t k h", t=128)),
              reads=["LFs"], writes=["LFs"], dma=True)
        mm(psS[0][0:NS, :], "psS0", lambda c: hTx[:, c, 0:NS], lambda c: W1b[:, c, 0:512], ["W1b", "hTx"])
        A("act", lambda e: e.activation(out=o32[0][0:NS, :], in_=psS[0][0:NS, :], func=AF.Copy), reads=["psS0"], writes=["o320"])
        A("pool", lambda e: e.dma_start(out=nks, in_=o32[0][0:NS, :]), reads=["o320"], dma=True)
        for b in range(SBC):
            mm(psS[2][0:16, 0:8], "psS2", lambda c, b=b: hTx[:, c, b * 16:(b + 1) * 16], lambda c: W1b[:, c, 1024:1032], ["W1b", "hTx"])
            A("dve", lambda e, b=b: e.tensor_tensor(out=Zs[0:16, b, :], in0=psS[2][0:16, 0:8], in1=bft[0:16, 0, :], op=ALU.add),
              reads=["psS2", "bft"], writes=["Zs"])
        A("act", lambda e: e.activation(out=Zs[:], in_=Zs[:], func=AF.Exp, scale=-1.0), reads=["Zs"], writes=["Zs"])
        A("act", lambda e: e.activation(out=Zs[:], in_=Zs[:], func=AF.Ln, bias=onec[0:16, :]), reads=["Zs", "onec"], writes=["Zs"])
        A("dve", lambda e: e.tensor_scalar(out=LFs[0:16, :, PB, :], in0=Zs[:], scalar1=-1.0, scalar2=None, op0=ALU.mult),
          reads=["Zs", "LFs"], writes=["LFs"])
        for b in range(SBC):
            A("pool", lambda e, b=b: e.dma_start(out=nlfs[b * 16:(b + 1) * 16, :], in_=LFs[0:16, b, PB, :]), reads=["LFs"], dma=True)
        ns_all = SBC * NBS * H
        LFsf = LFs[:].rearrange("p b k h -> p (b k h)")
        A("pe", lambda e: e.matmul(psO[0][:, 0:ns_all], lhsT=cstt[:, 128:256], rhs=LFsf, start=True, stop=True),
          reads=["cstt", "LFs"], writes=["psO0"])
        A("dve", lambda e: e.tensor_copy(out=CLs[:].rearrange("p b k h -> p (b k h)"), in_=psO[0][:, 0:ns_all]), reads=["psO0"], writes=["CLs"])
        A("pe", lambda e: e.matmul(psO[1][:, 0:ns_all], lhsT=onesf[:], rhs=LFsf, start=True, stop=True),
          reads=["onesf", "LFs"], writes=["psO1"])
        A("dve", lambda e: e.tensor_copy(out=TOTs[:].rearrange("p b k h -> p (b k h)"), in_=psO[1][:, 0:ns_all]), reads=["psO1"], writes=["TOTs"])
        A("pool", lambda e: e.memset(BPs[:], 0.0), writes=["BPs"])
        for k in range(PB):
            A("dve", lambda e, k=k: e.tensor_tensor(out=BPs[:, :, k + 1, :], in0=BPs[:, :, k, :], in1=TOTs[:, :, k, :], op=ALU.add),
              reads=["BPs", "TOTs"], writes=["BPs"])
        A("dve", lambda e: e.tensor_tensor(out=CLs[:], in0=CLs[:], in1=BPs[:], op=ALU.add), reads=["CLs", "BPs"], writes=["CLs"])
        A("dve", lambda e: e.tensor_scalar(out=NCs, in0=CLs[:], scalar1=-1.0, scalar2=None, op0=ALU.mult), reads=["CLs", "NC"], writes=["NCs", "NC"])
        A("dve", lambda e: e.tensor_scalar(out=c8s[:], in0=CLs[0:16, :, PB, :], scalar1=8.0, scalar2=None, op0=ALU.mult), reads=["CLs"], writes=["c8s"])
        split3(c8s[:], CPs[:], "c8s", "CPs", r1s[:], "r1s")
        nsk = SBC * NBS * H
        ck8s = vbuf[0].rearrange("p a b -> p (a b)").bitcast(F32)[:, 0:nsk].rearrange("p (b k h) -> p b k h", b=SBC, k=NBS)
        ckts = vbuf[0].rearrange("p a b -> p (a b)").bitcast(F32)[:, nsk:2 * nsk].rearrange("p (b k h) -> p b k h", b=SBC, k=NBS)
        CPks = vbuf[1].rearrange("p a b -> p (a b)")[:, 0:3 * nsk].rearrange("p (b k h r) -> p b k h r", b=SBC, k=NBS, h=H)
        A("dve", lambda e: e.tensor_scalar(out=ck8s, in0=NCs, scalar1=8.0, scalar2=None, op0=ALU.mult), reads=["NCs", "NC", "vb0"], writes=["vb0"])
        A("dve", lambda e: e.tensor_copy(out=CPks[:, :, :, :, 0], in_=ck8s), reads=["vb0", "vb1"], writes=["vb1"])
        A("dve", lambda e: e.tensor_tensor(out=ckts, in0=ck8s, in1=CPks[:, :, :, :, 0], op=ALU.subtract), reads=["vb0", "vb1"], writes=["vb0"])
        A("dve", lambda e: e.tensor_copy(out=CPks[:, :, :, :, 1], in_=ckts), reads=["vb0", "vb1"], writes=["vb1"])
        A("dve", lambda e: e.tensor_tensor(out=ckts, in0=ckts, in1=CPks[:, :, :, :, 1], op=ALU.subtract), reads=["vb0", "vb1"], writes=["vb0"])
        A("dve", lambda e: e.tensor_copy(out=CPks[:, :, :, :, 2], in_=ckts), reads=["vb0", "vb1"], writes=["vb1"])
        kst = rr[:].bitcast(BF16)
        for b in range(SBC):
            for g0 in range(0, NBS, 4):
                pa, na = bank7()
                gn = min(4, NBS - g0)
                for k in range(g0, g0 + gn):
                    A("pe", lambda e, k=k, b=b: e.matmul(pa[b * 32:b * 32 + 24, (k - g0) * 128:(k - g0 + 1) * 128],
                                                         lhsT=CPks[:, b, k, :, :].rearrange("p h r -> p (h r)"), rhs=identb[:],
                                                         start=True, stop=True, tile_position=(0, b * 32)),
                      reads=["vb1", "identb"], writes=[na])
                A("dve", lambda e, g0=g0, gn=gn, b=b: e.tensor_copy(out=kst[b * 32:b * 32 + 24, g0 * 128:(g0 + gn) * 128],
                                                                     in_=pa[b * 32:b * 32 + 24, 0:gn * 128]),
                  reads=[na, "rr0", "rr1"], writes=["rr0", "rr1"])
        hsrc_s = lambda c: hTx[:, c, 0:NS]
        A("pool", lambda e: e.memset(qa[64:128, :, 0:NS], 0.0), reads=["qa"], writes=["qa"])
        A("pool", lambda e: e.memset(qa[64:70, :, 0:NS], 1.0), reads=["qa"], writes=["qa"])
        for h in range(H):
            a = nxt("A", 2)
            mm(psA[a][0:64, 0:NS], f"psA{a}", lambda c, h=h: W2b[:, c, h * 64:(h + 1) * 64], hsrc_s, ["W2b", "hTx"])
            for b in range(SBC):
                A("pe", lambda e, a=a, h=h, b=b: e.matmul(psA[a][64:67, b * 16:(b + 1) * 16], lhsT=CPs[0:16, b, h, :],
                                                          rhs=identb[0:16, 0:16], start=True, stop=True, tile_position=(0, 64)),
                  reads=["CPs", "identb"], writes=[f"psA{a}"])
            A("dve", lambda e, a=a, h=h: e.tensor_copy(out=qa[0:67, h, 0:NS], in_=psA[a][0:67, 0:NS]), reads=[f"psA{a}"], writes=["qa"])
            b_ = nxt("A", 2)
            mm(psA[b_][0:64, 0:NS], f"psA{b_}", lambda c, h=h: W2b[:, c, 512 + h * 64:512 + (h + 1) * 64], hsrc_s, ["W2b", "hTx"])
            silu_gate(psA[b_], f"psA{b_}", 64, NS, sz[0:64, h, 0:NS], "sz")
        for j in range(4):
            def halo_s(j=j):
                A("pool", lambda e: e.tensor_copy(out=uts[:, :, 0:2], in_=sct[:, j, :, :]), reads=["sct"], writes=["ut"])
            conv_chunk(j, hsrc_s, ["hTx"], NS, uts[:], uts[:, :, 2:18], halo_s, acc[:, 0:NS].rearrange("p (b l) -> p b l", b=SBC),
                       catc[:, j, 0:NS])
        for b in range(SBC):
            u_last2(lambda c, b=b: hTx[:, c, b * 16 + 14:b * 16 + 16], ncs[b])
        for b in range(SBC):
            if b > 0:
                load_cache(b)
            s_ = nxt("S", 3)
            mm(psS[s_][0:16, :], f"psS{s_}", lambda c, b=b: hTx[:, c, b * 16:(b + 1) * 16], lambda c: W1b[:, c, 512:1024], ["W1b", "hTx"])
            A("dve", lambda e, s_=s_: e.tensor_copy(out=vcs[0:16, PB, :, 0:64], in_=psS[s_][0:16, :].rearrange("k (h d) -> k h d", h=H)),
              reads=[f"psS{s_}", "vcs"], writes=["vcs"])
            o = 0
            A("act", lambda e, s_=s_, o=o: e.activation(out=o32[o][0:16, :], in_=psS[s_][0:16, :], func=AF.Copy), reads=[f"psS{s_}"], writes=[f"o32{o}"])
            A("pool", lambda e, o=o, b=b: e.dma_start(out=nvs[b * 16:(b + 1) * 16, :], in_=o32[o][0:16, :]), reads=[f"o32{o}"], dma=True)
            pend_h = []

            def fin_head(hh, kb2, kn2, P2_, b=b):
                h2_ = hh % 2
                Qh = nxt("ph", 2)
                pd = pTT[0][:, Qh, :]
                pn = "pT0a" if Qh == 0 else "pT0b"
                A("act", lambda e: e.activation(out=pd[0:128, 0:PB * 16], in_=psSS[P2_][0:128, 0:PB * 16], func=AF.Exp, scale=0.125),
                  reads=[SNAMES[P2_][0]], writes=[pn])
                A("act", lambda e: e.activation(out=pd[0:16, PB * 16:NBS * 16], in_=psSS[P2_][0:16, PB * 16:NBS * 16], func=AF.Exp, scale=0.125),
                  reads=[SNAMES[P2_][0], pn], writes=[pn])
                for k in range(NBS):
                    nk = 128 if k < PB else 16
                    A("pe", lambda e, k=k, nk=nk: e.matmul(psO[h2_][0:65, b * 16:(b + 1) * 16], lhsT=vcs[0:nk, k, hh, 0:65],
                                                           rhs=pd[0:nk, k * 16:(k + 1) * 16], start=(k == 0), stop=(k == PB)),
                      reads=["vcs", pn], writes=[f"psO{h2_}"])
                attn_epilogue(h2_, hh, b * 16 + 16, c0=b * 16)

            for h in range(H):
                slot = nxt("kb", 2)
                h2 = h % 2
                kb_ = kbuf[slot][h2]
                kn = f"kb{slot}{h2}"
                A("sp", lambda e, kb_=kb_, h=h: e.dma_start(out=kb_[67:70, 0:P + 16], in_=kst[b * 32 + 3 * h:b * 32 + 3 * h + 3, 0:P + 16]),
                  reads=["rr0", "rr1"], writes=[kn + ".aug"], dma=True)
                for k in range(PB):
                    A("pe", lambda e, k=k, h=h: e.transpose(out=psT[0:64, k, :], in_=ckbs[b % 2][:, k, h * 64:(h + 1) * 64], identity=identb[:]),
                      reads=cknm[b % 2] + ["identb"], writes=["psT"])
                A("dve", lambda e, kb_=kb_: e.tensor_copy(out=kb_[0:64, 0:P].rearrange("p (k t) -> p k t", k=PB), in_=psT[0:64, 0:PB, :]),
                  reads=["psT"], writes=[kn])
                a = nxt("A", 2)
                mm(psA[a][0:64, 0:16], f"psA{a}", lambda c, h=h: W1b[:, c, h * 64:(h + 1) * 64], lambda c, b=b: hTx[:, c, b * 16:(b + 1) * 16],
                   ["W1b", "hTx"])
                A("dve", lambda e, a=a, kb_=kb_: e.tensor_copy(out=kb_[0:64, P:P + 16], in_=psA[a][0:64, 0:16]), reads=[f"psA{a}"], writes=[kn])
                P_ = nxt("S", 2)
                for k in range(NBS):
                    nk = 128 if k < PB else 16
                    A("pe", lambda e, k=k, nk=nk, kb_=kb_, h=h, P_=P_: e.matmul(psSS[P_][0:nk, k * 16:(k + 1) * 16], lhsT=kb_[0:128, k * 128:k * 128 + nk],
                                                                               rhs=qa[0:128, h, b * 16:(b + 1) * 16], start=True, stop=(k < PB)),
                      reads=[kn, kn + ".aug", "qa"], writes=[SNAMES[P_][0]])
                A("pe", lambda e, P_=P_: e.matmul(psSS[P_][0:16, PB * 16:NBS * 16], lhsT=identb[0:16, 0:16], rhs=trib[0:16, 0:16], start=False, stop=True),
                  reads=["identb", "trib"], writes=[SNAMES[P_][0]])
                pend_h.append((h, kb_, kn, P_))
                if len(pend_h) > 1:
                    fin_head(*pend_h.pop(0))
            while pend_h:
                fin_head(*pend_h.pop(0))
        out_proj(0, NS, xt[0], "xt0", ys)

    if upto is not None:
        pg.ops = pg.ops[:pg.marks[upto]]
    pg.emit()
    return nc


_SPL = dict(q=(0, 512), k=(512, 1024), v=(1024, 1536), za=(1536, 2048), fl=(2048, 2056), B=(2056, 2568), C=(2568, 3080),
            hc=(3080, 3592), zc=(3592, 4104))


def _prep_common(norm_g, w_in, b_f, conv_w, w_out, final_g):
    w = np.asarray(w_in[0], np.float32)
    cols = lambda *ks: np.concatenate([w[:, _SPL[k][0]:_SPL[k][1]] for k in ks], axis=1)
    pcn = lambda a: np.ascontiguousarray(a.reshape(KC, 128, a.shape[1]).transpose(1, 0, 2))
    wo = np.asarray(w_out[0], np.float32)
    U = np.triu(np.ones((128, 128), np.float32))
    tri = np.where(np.arange(128)[:, None] <= np.arange(128)[None, :], 0.0, NEG).astype(np.float32)
    mrow = np.zeros((128, 1), np.float32); mrow[:16] = 1.0
    return dict(
        w1=pcn(cols("k", "v", "fl")), w2=pcn(cols("q", "za", "B", "C", "hc", "zc")),
        woa=np.ascontiguousarray(wo[0:512].reshape(H, 64, D).transpose(1, 0, 2)),
        woc=np.ascontiguousarray(wo[512:1024].reshape(4, 128, D).transpose(1, 0, 2)),
        gcol=np.ascontiguousarray(np.asarray(norm_g[0], np.float32).reshape(KC, 128).T),
        fg=np.ascontiguousarray(np.tile(np.asarray(final_g, np.float32)[None, :], (128, 1))),
        bf=np.ascontiguousarray(np.tile(np.asarray(b_f[0], np.float32)[None, None, :], (128, 4, 1))),
        cw=np.ascontiguousarray(np.asarray(conv_w[0], np.float32).reshape(3, 4, 128).transpose(2, 1, 0)),
        cst=np.ascontiguousarray(np.concatenate([np.eye(128, dtype=np.float32), U, tri], axis=1)),
        mrow=mrow)


def _perm(r):
    return [j for j in range(4) if j != r] + [r]


def _core_inputs(common, r, xb, meta_tokens, NB):
    NG = NB // 4
    pm = _perm(r)
    x4 = xb.reshape(NG, 4, 128, D)
    xa = np.ascontiguousarray(x4[:, pm].reshape(NB * 128, D))
    xp = np.concatenate([meta_tokens, xb], axis=0)
    rows = []
    for m in range(NG):
        p0 = 16 + (4 * m + r) * 128
        rows.append(xp[p0 - 2:p0])
    rows.append(xp[-2:])
    xh = np.ascontiguousarray(np.concatenate(rows, axis=0))
    tri = common["cst"][:, 256:384]
    msk = np.zeros((128, 4, 128), np.float32)
    for jj in range(4):
        j = pm[jj]
        if j == r:
            msk[:, jj, :] = tri
        elif j > r:
            msk[:, jj, :] = NEG
    w4 = np.zeros((16,), np.float32)
    for j2 in range(4):
        for jj in range(4):
            w4[j2 * 4 + jj] = 1.0 if pm[j2] < pm[jj] else 0.0
    d = dict(common)
    d.update(xa=xa, meta=np.ascontiguousarray(meta_tokens), xh=xh, msk=np.ascontiguousarray(msk.reshape(128, 512)),
             w4=np.ascontiguousarray(np.tile(w4[None, :], (128, 1))))
    return d


_NC_CACHE = {}


def kernel(x_prompt, x_sample, cache_k, cache_v, cache_logf, state_conv, meta_tokens,
           norm_g, w_in, b_f, conv_w, w_out, final_g, _runner=None, _upto=None):
    f = lambda a: np.asarray(a, np.float32)
    x_prompt, x_sample, cache_k, cache_v, cache_logf, state_conv, meta_tokens = map(
        f, (x_prompt, x_sample, cache_k, cache_v, cache_logf, state_conv, meta_tokens))
    B, SEQ, _ = x_prompt.shape
    NB = SEQ // 128
    NG = NB // 4
    DB, S, _ = x_sample.shape
    P = cache_k.shape[2]
    n_cores = 4 * B
    SBC = DB // n_cores
    key = (NB, SBC, P, _upto)
    if key not in _NC_CACHE:
        _NC_CACHE[key] = build(NB=NB, SBC=SBC, P=P, do_sample=True, upto=_upto)
    nc = _NC_CACHE[key]
    common = _prep_common(f(norm_g), f(w_in), f(b_f), f(conv_w), f(w_out), f(final_g))
    in_maps = []
    for core in range(n_cores):
        b, r = divmod(core, 4)
        d = _core_inputs(common, r, x_prompt[b], meta_tokens, NB)
        sl = slice(core * SBC, (core + 1) * SBC)
        d["xs"] = np.ascontiguousarray(x_sample[sl].reshape(SBC * 16, D))
        d["ck"] = np.ascontiguousarray(cache_k[0, sl].reshape(SBC, P, 512))
        d["cv"] = np.ascontiguousarray(cache_v[0, sl].reshape(SBC, P, 512))
        d["clf"] = np.ascontiguousarray(cache_logf[0, sl])
        d["scT"] = np.ascontiguousarray(state_conv[0, sl].reshape(SBC, 2, 4, 128).transpose(3, 2, 0, 1))
        in_maps.append(d)
    if _runner is not None:
        results = _runner(nc, in_maps)
    else:
        results = run_bass_kernel_spmd(nc, in_maps, core_ids=list(range(n_cores))).results
    L = 16 + SEQ
    y_prompt = np.zeros((B, SEQ, D), np.float32)
    nk = np.zeros((1, B, L, H, 64), np.float32); nv = np.zeros((1, B, L, H, 64), np.float32)
    nlf = np.zeros((1, B, L, H), np.float32); ncp = np.zeros((1, B, 2, 512), np.float32)
    y_sample = np.zeros((DB, S, D), np.float32)
    nks = np.zeros((1, DB, S, H, 64), np.float32); nvs = np.zeros((1, DB, S, H, 64), np.float32)
    nlfs = np.zeros((1, DB, S, H), np.float32); ncs = np.zeros((1, DB, 2, 512), np.float32)
    for core in range(n_cores):
        b, r = divmod(core, 4)
        res = results[core]
        for m in range(NG):
            t0 = (4 * m + r) * 128
            y_prompt[b, t0:t0 + 128] = res["y_own"][m * 128:(m + 1) * 128]
            nk[0, b, 16 + t0:16 + t0 + 128] = res["nk_own"][m * 128:(m + 1) * 128].reshape(128, H, 64)
            nv[0, b, 16 + t0:16 + t0 + 128] = res["nv_own"][m * 128:(m + 1) * 128].reshape(128, H, 64)
            nlf[0, b, 16 + t0:16 + t0 + 128] = res["nlf_own"][m * 128:(m + 1) * 128]
        if r == 0:
            nk[0, b, 0:16] = res["nk_m"].reshape(16, H, 64); nv[0, b, 0:16] = res["nv_m"].reshape(16, H, 64)
            nlf[0, b, 0:16] = res["nlf_m"]; ncp[0, b] = res["ncv"]
        sl = slice(core * SBC, (core + 1) * SBC)
        y_sample[sl] = res["ys"].reshape(SBC, 16, D)
        nks[0, sl] = res["nks"].reshape(SBC, 16, H, 64); nvs[0, sl] = res["nvs"].reshape(SBC, 16, H, 64)
        nlfs[0, sl] = res["nlfs"].reshape(SBC, 16, H); ncs[0, sl] = res["ncs"]
    return (y_prompt, y_sample, nk, nv, nlf, ncp, nks, nvs, nlfs, ncs)
```

```python
import numpy as np
import concourse.bass as bass
import concourse.mybir as mybir
from concourse.bass_utils import run_bass_kernel_spmd

F32 = mybir.dt.float32
BF16 = mybir.dt.bfloat16
AF = mybir.ActivationFunctionType
ALU = mybir.AluOpType
ENGINES = ("pe", "act", "dve", "pool", "sp")
D = 1024
KC = 8
H = 8
EPS = 1e-6
NEG = -30000.0


class _Rec:
    def __getattr__(self, name):
        return lambda *a, **k: (name, a, k)


_REC = _Rec()


class Prog:
    def __init__(self, nc, n_dma_sems=8):
        self.nc = nc
        self.ops = []
        self.last_write = {}
        self.readers = {}
        self.n_dma_sems = n_dma_sems
        self.marks = {}

    def mark(self, name):
        self.marks[name] = len(self.ops)

    def op(self, eng, fn, reads=(), writes=(), dma=False):
        i = len(self.ops)
        deps = set()
        writes = list(writes) + [r for r in reads if r.startswith("ps") and r not in writes]
        for r in reads:
            if r in self.last_write:
                deps.add(self.last_write[r])
        for w in writes:
            if w in self.last_write:
                deps.add(self.last_write[w])
            for rd in self.readers.get(w, ()):
                deps.add(rd)
        self.ops.append(dict(eng=eng, fn=fn(_REC), deps=deps, dma=dma))
        for r in reads:
            self.readers.setdefault(r, []).append(i)
        for w in writes:
            self.last_write[w] = i
            self.readers[w] = []
        return i

    def emit(self):
        nc = self.nc
        ops = self.ops
        needed = [False] * len(ops)
        for i, o in enumerate(ops):
            for d in o["deps"]:
                p = ops[d]
                if p["eng"] == "pe" and o["eng"] == "pe" and not p["dma"] and not o["dma"]:
                    continue
                needed[d] = True
        csem = {e: nc.alloc_semaphore(name=f"c_{e}") for e in ENGINES}
        dsem = {e: [nc.alloc_semaphore(name=f"d_{e}{k}") for k in range(self.n_dma_sems)] for e in ENGINES
                if any(o["dma"] and o["eng"] == e for o in ops)}
        ccount = {e: 0 for e in ENGINES}
        dcount = {e: [0] * self.n_dma_sems for e in dsem}
        drr = {e: 0 for e in dsem}
        sig = [None] * len(ops)
        pre_wait = [None] * len(ops)
        for i, o in enumerate(ops):
            e = o["eng"]
            if o["dma"]:
                k = drr[e]
                drr[e] = (k + 1) % self.n_dma_sems
                prev = dcount[e][k]
                if prev > 0:
                    pre_wait[i] = (dsem[e][k], prev)
                dcount[e][k] = prev + 16
                sig[i] = (dsem[e][k], prev + 16)
            elif needed[i]:
                ccount[e] += 1
                sig[i] = (csem[e], ccount[e])
        streams = {e: [] for e in ENGINES}
        waited = {e: {} for e in ENGINES}
        for i, o in enumerate(ops):
            e = o["eng"]
            cand = []
            if pre_wait[i] is not None:
                cand.append(pre_wait[i])
            for d in sorted(o["deps"]):
                if sig[d] is None:
                    continue
                p = ops[d]
                if p["eng"] == "pe" and e == "pe" and not p["dma"] and not o["dma"]:
                    continue
                cand.append(sig[d])
            best = {}
            for (s, v) in cand:
                key = id(s)
                if key not in best or best[key][1] < v:
                    best[key] = (s, v)
            waits = []
            for key, (s, v) in best.items():
                if waited[e].get(key, 0) >= v:
                    continue
                waited[e][key] = v
                waits.append((s, v))
            streams[e].append((i, waits))
        final_waits = []
        for e in dsem:
            for k in range(self.n_dma_sems):
                if dcount[e][k] > 0:
                    final_waits.append((dsem[e][k], dcount[e][k]))
        self.stats = dict(n_ops=len(ops), n_sig=sum(1 for s in sig if s is not None),
                          n_waits=sum(len(w) for e in ENGINES for _, w in streams[e]))

        def run(e, eng):
            for (i, waits) in streams[e]:
                for (s, v) in waits:
                    eng.wait_ge(s, v)
                nm_, a_, k_ = ops[i]["fn"]
                ins = getattr(eng, nm_)(*a_, **k_)
                if sig[i] is not None:
                    ins.then_inc(sig[i][0], 16 if ops[i]["dma"] else 1)
            if e == "sp":
                for (s, v) in final_waits:
                    eng.wait_ge(s, v)

        with nc.Block() as block:
            @block.tensor
            def _(eng):
                run("pe", eng)

            @block.scalar
            def _(eng):
                run("act", eng)

            @block.vector
            def _(eng):
                run("dve", eng)

            @block.gpsimd
            def _(eng):
                run("pool", eng)

            @block.sync
            def _(eng):
                run("sp", eng)


def build(NB=64, SBC=2, P=1024, do_sample=True, upto=None):
    NG = NB // 4
    NSB = NG // 4
    NBLK1 = NB + 1
    TT = 16 + NB * 128
    NH = 2 * NG + 2
    PB = P // 128
    NS = SBC * 16
    nc = bass.Bass("TRN2", target_bir_lowering=False)
    pg = Prog(nc)

    def din(name, shape, dt=F32):
        return nc.dram_tensor(name, list(shape), dt, kind="ExternalInput").ap()

    def dout(name, shape, dt=F32):
        return nc.dram_tensor(name, list(shape), dt, kind="ExternalOutput").ap()

    xa = din("xa", [NB * 128, D]); meta = din("meta", [16, D]); xh = din("xh", [NH, D])
    w1 = din("w1", [128, KC, 1032]); w2 = din("w2", [128, KC, 3072])
    woa = din("woa", [64, H, D]); woc = din("woc", [128, 4, D])
    gcol = din("gcol", [128, KC]); fg = din("fg", [128, D]); bfi = din("bf", [128, 4, H])
    cwi = din("cw", [128, 4, 3]); cst = din("cst", [128, 3 * 128]); mski = din("msk", [128, 4 * 128])
    w4i = din("w4", [128, 16]); mrow = din("mrow", [128, 1])
    y_own = dout("y_own", [NG * 128, D]); nk_own = dout("nk_own", [NG * 128, 512])
    nv_own = dout("nv_own", [NG * 128, 512]); nlf_own = dout("nlf_own", [NG * 128, H])
    nk_m = dout("nk_m", [16, 512]); nv_m = dout("nv_m", [16, 512]); nlf_m = dout("nlf_m", [16, H])
    ncv = dout("ncv", [2, 512])
    if do_sample:
        xs = din("xs", [NS, D]); cki = din("ck", [SBC, P, 512]); cvi = din("cv", [SBC, P, 512])
        clf = din("clf", [SBC, P, H]); scT = din("scT", [128, 4, SBC, 2])
        ys = dout("ys", [NS, D]); nks = dout("nks", [NS, 512]); nvs = dout("nvs", [NS, 512])
        nlfs = dout("nlfs", [NS, H]); ncs = dout("ncs", [SBC, 2, 512])
    kT_scr = nc.dram_tensor("kT_scr", [4, 128, TT], BF16, kind="Internal").ap()
    kaug_scr = nc.dram_tensor("kaug_scr", [H * 3, TT], BF16, kind="Internal").ap()
    v_scr = nc.dram_tensor("v_scr", [4, 128, NBLK1, 132], BF16, kind="Internal").ap()

    def sb(name, shape, dt=F32):
        return nc.alloc_sbuf_tensor(name, list(shape), dt)

    psA = [nc.alloc_psum_tensor(f"psA{i}", [128, 512], F32) for i in range(2)]
    psSS = [nc.alloc_psum_tensor(f"psSS{i}", [128, 1024], F32) for i in range(2)]
    psS = [psSS[0][:, 0:512], psSS[0][:, 512:1024], psSS[1][:, 0:512]]
    psT = psSS[1][:, 512:1024].bitcast(BF16).rearrange("p (c t) -> p c t", c=KC)
    SNAMES = [["psS0", "psS1"], ["psS2", "psT"]]
    psO = [nc.alloc_psum_tensor(f"psO{i}", [128, 512], F32) for i in range(2)]
    W1b = sb("W1b", [128, KC, 1032], BF16); W2b = sb("W2b", [128, KC, 3072], BF16)
    woab = sb("woab", [128, H, D], BF16); wocb = sb("wocb", [128, 4, D], BF16)
    gct = sb("gct", [128, KC]); fgt = sb("fgt", [128, D]); bft = sb("bft", [128, 4, H]); cwt = sb("cwt", [128, 4, 3])
    cstt = sb("cstt", [128, 384]); mskt = sb("mskt", [128, 512]); w4t = sb("w4t", [128, 16]); mrt = sb("mrt", [128, 1])
    identb = sb("identb", [128, 128], BF16); onesb = sb("onesb", [128, 128], BF16)
    maskb = sb("maskb", [128, 4, 128], BF16); trib = sb("trib", [128, 128], BF16)
    xt = [sb(f"xt{i}", [128, D]) for i in range(3)]
    xsb = [sb(f"xsb{i}", [128, D], BF16) for i in range(2)]
    ssq = sb("ssq", [128, 8]); rst = sb("rst", [128, 8])
    hT = [sb(f"hT{i}", [128, KC, 512], BF16) for i in range(2)]
    hTx = sb("hTx", [128, KC, 64], BF16)
    o32 = [sb("o320", [128, 512])] * 2
    NC = sb("NC", [128, NBLK1, H]); CPt = sb("CPt", [128, NG, H, 3], BF16)
    uh = sb("uh", [128, 4, NH])
    KCH = 8
    KCOL = 16 + KCH * 128
    kbuf = [[sb(f"kb{i}{j}", [128, KCOL], BF16) for j in range(2)] for i in range(2)]
    vbuf = [sb(f"vb{i}", [128, KCH + 1, 132], BF16) for i in range(2)]
    pTT = [sb("pTT0", [128, 2, 512], BF16), hT[1][:, 0:2, :]]
    PNAMES = ["pT0", "pT2"]
    rr = sb("rr", [128, D])
    o3 = [(o32[0][:], "o320"), (rr[:, 0:512], "rr0"), (rr[:, 512:1024], "rr1")]
    nb8 = NBLK1 * H
    NREG = max(2048 + 2112 + 264 + 2064 + 5 * nb8 + 2 * (NG + 1) * H + 2 * NG * H, 3 * 2048 + 1024 + 3 * 512 + 520 + 512) + 64
    REG = sb("REG", [128, NREG])
    _o = [0]

    def rv(nwords, dt=F32, pattern=None, **kw):
        a = REG[:, _o[0]:_o[0] + nwords]
        _o[0] += nwords
        if dt == BF16:
            a = a.bitcast(BF16)
        if pattern:
            a = a.rearrange(pattern, **kw)
        return a

    ktsb = [rv(1024, BF16, "p (a b) -> p a b", a=4) for i in range(2)]
    vsb = [rv(1056, BF16, "p (a b c d) -> p a b c d", a=4, b=4, c=2) for i in range(2)]
    vmt = rv(264, BF16, "p (a b c d) -> p a b c d", a=4, b=1, c=2)
    wst = [rv(1032) for i in range(2)]
    Z = rv(nb8, F32, "p (b h) -> p b h", h=H); LF = rv(nb8, F32, "p (b h) -> p b h", h=H)
    CL = rv(nb8, F32, "p (b h) -> p b h", h=H); TOT = rv(nb8, F32, "p (b h) -> p b h", h=H)
    BP = rv(nb8, F32, "p (b h) -> p b h", h=H)
    sa = [rv((NG + 1) * H, F32, "p (b h) -> p b h", h=H) for i in range(2)]
    c8 = rv(NG * H, F32, "p (b h) -> p b h", h=H); r1 = rv(NG * H, F32, "p (b h) -> p b h", h=H)
    P1_NAMES = ["ktsb0", "ktsb1", "ktsb1.1", "ktsb1.2", "ktsb1.3", "vsb0", "vsb1", "vmt", "wst0", "wst1", "Z", "LF", "CL", "TOT", "BP", "sa0", "sa1", "c8", "r1"]
    _o[0] = 0
    qa = rv(2048, BF16, "p (a b) -> p a b", a=H); sz = rv(2048, BF16, "p (a b) -> p a b", a=H)
    ca = rv(2048, BF16, "p (a b) -> p a b", a=H); catc = rv(1024, BF16, "p (a b) -> p a b", a=4)
    t5 = [rv(512) for i in range(3)]
    ut = rv(520, F32, "p (a b) -> p a b", a=4); acc = rv(512)
    rdh_v = t5[1][:, 0:256].bitcast(BF16); rdl_v = t5[1][:, 256:512].bitcast(BF16)
    P2_NAMES = ["qa", "sz", "ca", "catc", "t50", "t51", "t52", "ut", "acc"]

    onesf = sb("onesf", [128, 128]); epst = sb("epst", [128, 1]); onec = sb("onec", [128, 1])
    A = pg.op
    cnt = {"x": 0, "A": 0, "S": 0, "p": 0, "y": 0, "o": 0, "xs": 0, "w": 0, "kb": 0, "st": 0, "B": 0, "ph": 0}

    def nxt(k, n):
        v = cnt[k] % n
        cnt[k] += 1
        return v

    for (t, src, nm) in ((gct, gcol, "gct"), (fgt, fg, "fgt"), (bft, bfi, "bft"), (cwt, cwi, "cwt"),
                         (cstt, cst, "cstt"), (mskt, mski, "mskt"), (w4t, w4i, "w4t"), (mrt, mrow, "mrt")):
        A("sp", lambda e, t=t, src=src: e.dma_start(out=t[:], in_=src), writes=[nm], dma=True)
    A("dve", lambda e: e.tensor_scalar(out=cwt[:], in0=cwt[:], scalar1=0.5, scalar2=None, op0=ALU.mult), reads=["cwt"], writes=["cwt"])
    A("dve", lambda e: e.tensor_copy(out=identb[:], in_=cstt[:, 0:128]), reads=["cstt"], writes=["identb"])
    A("dve", lambda e: e.tensor_copy(out=trib[:], in_=cstt[:, 256:384]), reads=["cstt"], writes=["trib"])
    A("dve", lambda e: e.tensor_copy(out=maskb[:], in_=mskt[:].rearrange("p (a b) -> p a b", a=4)),
      reads=["mskt"], writes=["maskb"])
    A("pool", lambda e: e.memset(woab[64:128], 0.0), writes=["woab"])
    A("pool", lambda e: e.memset(onesb[:], 1.0), writes=["onesb"])
    A("pool", lambda e: e.memset(onesf[:], 1.0), writes=["onesf"])
    A("pool", lambda e: e.memset(epst[:], EPS), writes=["epst"])
    A("pool", lambda e: e.memset(onec[:], 1.0), writes=["onec"])
    for i in range(2):
        A("pool", lambda e, i=i: e.memset(vsb[i][:], 2.0), writes=[f"vsb{i}"])
        for j in range(2):
            A("pool", lambda e, i=i, j=j: e.memset(kbuf[i][j][:], 1.0), writes=[f"kb{i}{j}"])
    A("pool", lambda e: e.memset(vmt[:], 2.0), writes=["vmt"])
    A("pool", lambda e: e.memset(ssq[:], 0.0), writes=["ssq0", "ssq1", "ssq2", "ssq3"])

    wq = []
    wstate = {"issued": 0, "done": 0}

    def load_w(dst, src, ncols, nm, chunks, scaled, np_=128):
        for c in range(chunks):
            for x0 in range(0, ncols, 1024):
                x1 = min(ncols, x0 + 1024) if ncols != 1032 else 1032
                wq.append(dict(dst=dst, src=src, c=c, x0=x0, x1=x1, nm=nm, scaled=scaled, np_=np_))
                if ncols == 1032:
                    break

    def _w_dma(i):
        w_ = wq[i]
        k = i % 2
        A("act", lambda e: e.dma_start(out=wst[k][0:w_["np_"], 0:w_["x1"] - w_["x0"]], in_=w_["src"][:, w_["c"], w_["x0"]:w_["x1"]]),
          writes=[f"wst{k}"], dma=True)

    def emit_w(n):
        for _ in range(n):
            if wstate["done"] >= len(wq):
                return
            while wstate["issued"] < len(wq) and wstate["issued"] <= wstate["done"] + 1:
                _w_dma(wstate["issued"])
                wstate["issued"] += 1
            i = wstate["done"]
            w_ = wq[i]
            k = i % 2
            if w_["scaled"]:
                A("act", lambda e, w_=w_, k=k: e.activation(out=w_["dst"][0:w_["np_"], w_["c"], w_["x0"]:w_["x1"]],
                                                           in_=wst[k][0:w_["np_"], 0:w_["x1"] - w_["x0"]], func=AF.Copy,
                                                           scale=gct[0:w_["np_"], w_["c"]:w_["c"] + 1]),
                  reads=[f"wst{k}", "gct"], writes=[w_["nm"]])
            else:
                A("act", lambda e, w_=w_, k=k: e.activation(out=w_["dst"][0:w_["np_"], w_["c"], w_["x0"]:w_["x1"]],
                                                           in_=wst[k][0:w_["np_"], 0:w_["x1"] - w_["x0"]], func=AF.Copy),
                  reads=[f"wst{k}"], writes=[w_["nm"]])
            wstate["done"] += 1
            if wstate["issued"] < len(wq) and wstate["issued"] <= wstate["done"] + 1:
                _w_dma(wstate["issued"])
                wstate["issued"] += 1

    pg.mark("setup")
    load_w(W1b, w1, 1032, "W1b", KC, True)
    _w_dma(0)
    _w_dma(1)
    wstate["issued"] = 2
    pg.mark("w1")

    def norm_pre(xtile, xname, nt):
        k = nxt("xs", 2)
        q = nxt("st", 4)
        A("act", lambda e: e.activation(out=xsb[k][0:nt, :], in_=xtile[0:nt, :], func=AF.Square, accum_out=ssq[0:nt, q:q + 1]),
          reads=[xname, f"ssq{q}"], writes=[f"xsb{k}", f"ssq{q}"])
        A("act", lambda e: e.activation(out=rst[0:nt, 2 * q:2 * q + 1], in_=ssq[0:nt, q:q + 1], func=AF.Ln, scale=1.0 / D, bias=epst[0:nt, :]),
          reads=[f"ssq{q}", "epst"], writes=[f"rst{q}"])
        A("act", lambda e: e.activation(out=rst[0:nt, 2 * q + 1:2 * q + 2], in_=rst[0:nt, 2 * q:2 * q + 1], func=AF.Exp, scale=-0.5),
          reads=[f"rst{q}"], writes=[f"rst{q}"])
        A("pool", lambda e: e.memset(ssq[:, q:q + 1], 0.0), reads=[f"ssq{q}"], writes=[f"ssq{q}"])
        A("dve", lambda e: e.tensor_scalar(out=xsb[k][0:nt, :], in0=xtile[0:nt, :], scalar1=rst[0:nt, 2 * q + 1:2 * q + 2], scalar2=None,
                                           op0=ALU.mult),
          reads=[xname, f"rst{q}"], writes=[f"xsb{k}"])
        return k

    def norm_post(k, nt, dst, dname, evac_eng="act"):
        for c in range(KC):
            A("pe", lambda e, c=c: e.transpose(out=psT[:, c, 0:nt], in_=xsb[k][0:nt, c * 128:(c + 1) * 128],
                                               identity=identb[0:nt, 0:nt]),
              reads=[f"xsb{k}", "identb"], writes=["psT"])
        if evac_eng == "act":
            A("act", lambda e: e.activation(out=dst, in_=psT[:, :, 0:nt], func=AF.Copy), reads=["psT"], writes=[dname])
        else:
            A("dve", lambda e: e.tensor_copy(out=dst, in_=psT[:, :, 0:nt]), reads=["psT"], writes=[dname])

    def norm_T(xtile, xname, nt, dst, dname, evac_eng="act"):
        k = norm_pre(xtile, xname, nt)
        norm_post(k, nt, dst, dname, evac_eng)

    def p1_pre(m_, jj_):
        k = nxt("x", 3)
        blk = 4 * m_ + jj_
        A("sp", lambda e: e.dma_start(out=xt[k][:], in_=xa[blk * 128:(blk + 1) * 128, :]), writes=[f"xt{k}"], dma=True)
        return norm_pre(xt[k], f"xt{k}", 128)

    def p1_post(m_, jj_, kx):
        norm_post(kx, 128, hT[m_ % 2][:, :, jj_ * 128:(jj_ + 1) * 128], f"hT{m_ % 2}.{jj_}", evac_eng="dve")

    for jj in range(4):
        p1_post(0, jj, p1_pre(0, jj))
    emit_w(len(wq))

    def mm(ps, psn, lhs_fn, rhs_fn, rnames):
        for c in range(KC):
            A("pe", lambda e, c=c: e.matmul(ps, lhsT=lhs_fn(c), rhs=rhs_fn(c), start=(c == 0), stop=(c == KC - 1)),
              reads=list(rnames), writes=[psn])

    def final_norm_store(src_ps_pair, xres, xname, nt, dst_dram):
        for hf in range(2):
            A("dve", lambda e, hf=hf: e.tensor_tensor(out=rr[0:nt, hf * 512:(hf + 1) * 512], in0=src_ps_pair[hf][0][0:nt, :],
                                                      in1=xres[0:nt, hf * 512:(hf + 1) * 512], op=ALU.add),
              reads=[src_ps_pair[hf][1], xname], writes=[f"rr{hf}"])
        q = nxt("st", 4)
        k = nxt("xs", 2)
        A("act", lambda e: e.activation(out=xsb[k][0:nt, :], in_=rr[0:nt, :], func=AF.Square, accum_out=ssq[0:nt, q:q + 1]),
          reads=["rr0", "rr1", f"ssq{q}"], writes=[f"xsb{k}", f"ssq{q}"])
        A("act", lambda e: e.activation(out=rst[0:nt, 2 * q:2 * q + 1], in_=ssq[0:nt, q:q + 1], func=AF.Ln, scale=1.0 / D, bias=epst[0:nt, :]),
          reads=[f"ssq{q}", "epst"], writes=[f"rst{q}"])
        A("act", lambda e: e.activation(out=rst[0:nt, 2 * q + 1:2 * q + 2], in_=rst[0:nt, 2 * q:2 * q + 1], func=AF.Exp, scale=-0.5),
          reads=[f"rst{q}"], writes=[f"rst{q}"])
        A("pool", lambda e: e.memset(ssq[:, q:q + 1], 0.0), reads=[f"ssq{q}"], writes=[f"ssq{q}"])
        A("dve", lambda e: e.scalar_tensor_tensor(out=xres[0:nt, :], in0=rr[0:nt, :], scalar=rst[0:nt, 2 * q + 1:2 * q + 2], in1=fgt[0:nt, :],
                                                  op0=ALU.mult, op1=ALU.mult),
          reads=["rr0", "rr1", f"rst{q}", "fgt", xname], writes=[xname])
        A("pool", lambda e: e.dma_start(out=dst_dram, in_=xres[0:nt, :]), reads=[xname], dma=True)

    A("sp", lambda e: e.dma_start(out=xt[0][0:16, :], in_=meta), writes=["xt0"], dma=True)
    norm_T(xt[0], "xt0", 16, hTx[:, :, 0:16], "hTx")
    pg.mark("m1")
    for p in range(4):
        a = nxt("A", 2)
        mm(psA[a][:, 0:16], f"psA{a}", lambda c, p=p: W1b[:, c, p * 128:(p + 1) * 128], lambda c: hTx[:, c, 0:16], ["W1b", "hTx"])
        A("dve", lambda e, a=a, p=p: e.tensor_copy(out=ktsb[0][:, p, 0:16], in_=psA[a][:, 0:16]), reads=[f"psA{a}"], writes=["ktsb0"])
    A("pool", lambda e: e.dma_start(out=kT_scr[:, :, 0:16].rearrange("p r t -> r p t"), in_=ktsb[0][:, :, 0:16]),
      reads=["ktsb0"], writes=["kT_scr"], dma=True)
    pg.mark("m2")
    mm(psS[0][0:16, :], "psS0", lambda c: hTx[:, c, 0:16], lambda c: W1b[:, c, 512:1024], ["W1b", "hTx"])
    pg.mark("m2x")
    A("dve", lambda e: e.tensor_copy(out=vmt[0:16, :, 0, :, 0:64], in_=psS[0][0:16, :].rearrange("k (p h d) -> k p h d", p=4, h=2)),
      reads=["psS0"], writes=["vmt"])
    pg.mark("m2a")
    A("act", lambda e: e.activation(out=o32[0][0:16, :], in_=psS[0][0:16, :], func=AF.Copy), reads=["psS0"], writes=["o320"])
    A("pool", lambda e: e.dma_start(out=nv_m, in_=o32[0][0:16, :]), reads=["o320"], dma=True)
    pg.mark("m2b")
    A("pool", lambda e: e.dma_start(out=v_scr[:, :, 0:1, :].rearrange("p k b c -> k p b c"),
                                    in_=vmt[:].rearrange("k p b h c -> k p b (h c)")),
      reads=["vmt"], writes=["v_scr"], dma=True)
    pg.mark("m3")
    mm(psS[1][0:16, :], "psS1", lambda c: hTx[:, c, 0:16], lambda c: W1b[:, c, 0:512], ["W1b", "hTx"])
    A("act", lambda e: e.activation(out=o32[0][0:16, :], in_=psS[1][0:16, :], func=AF.Copy), reads=["psS1"], writes=["o320"])
    A("pool", lambda e: e.dma_start(out=nk_m, in_=o32[0][0:16, :]), reads=["o320"], dma=True)
    A("pool", lambda e: e.memset(Z[:, 0, :], 0.0), writes=["Z"])
    mm(psS[2][0:16, 0:8], "psS2", lambda c: hTx[:, c, 0:16], lambda c: W1b[:, c, 1024:1032], ["W1b", "hTx"])
    A("dve", lambda e: e.tensor_tensor(out=Z[0:16, 0, :], in0=psS[2][0:16, 0:8], in1=bft[0:16, 0, :], op=ALU.add),
      reads=["psS2", "bft"], writes=["Z"])

    pg.mark("meta")
    w2_loaded = False
    fut = {"pre": 0, "post": 0, "k": {}}
    NFUT = 4 * (NG - 1)

    def fut_pre(upto):
        while fut["pre"] < NFUT and fut["pre"] <= upto:
            i = fut["pre"]
            fut["k"][i] = p1_pre(1 + i // 4, i % 4)
            fut["pre"] += 1

    def fut_post(i):
        if i < NFUT:
            p1_post(1 + i // 4, i % 4, fut["k"].pop(i))
            fut["post"] += 1

    load_w(W2b, w2, 3072, "W2b", KC, True)
    load_w(woab, woa, D, "woab", H, False, np_=64)
    load_w(wocb, woc, D, "wocb", 4, False)
    for m in range(NG):
        par = m % 2
        hnames = [f"hT{par}.{j}" for j in range(4)]
        def v_block(jj, m=m, par=par):
                s_ = nxt("S", 2)
                mm(psS[s_][:, :], f"psS{s_}", lambda c, jj=jj: hT[par][:, c, jj * 128:(jj + 1) * 128],
                   lambda c: W1b[:, c, 512:1024], ["W1b", f"hT{par}.{jj}"])
                A("dve", lambda e, s_=s_, jj=jj: e.tensor_copy(out=vsb[par][:, :, jj, :, 0:64],
                                                              in_=psS[s_][:, :].rearrange("k (p h d) -> k p h d", p=4, h=2)),
                  reads=[f"psS{s_}"], writes=[f"vsb{par}"])
                if jj == 3:
                    ot, on = o3[nxt("o", 3)]
                    A("act", lambda e, s_=s_: e.activation(out=ot, in_=psS[s_][:, :], func=AF.Copy),
                      reads=[f"psS{s_}"], writes=[on])
                    A("pool", lambda e, m=m: e.dma_start(out=nv_own[m * 128:(m + 1) * 128, :], in_=ot),
                      reads=[on], dma=True)
                    s2 = nxt("S", 2)
                    mm(psS[s2][:, :], f"psS{s2}", lambda c: hT[par][:, c, 384:512], lambda c: W1b[:, c, 0:512],
                       ["W1b", f"hT{par}.3"])
                    ot2, on2 = o3[nxt("o", 3)]
                    A("act", lambda e, s2=s2: e.activation(out=ot2, in_=psS[s2][:, :], func=AF.Copy),
                      reads=[f"psS{s2}"], writes=[on2])
                    A("pool", lambda e, m=m: e.dma_start(out=nk_own[m * 128:(m + 1) * 128, :], in_=ot2),
                      reads=[on2], dma=True)

        for p in range(4):
            fut_pre(4 * m + p + 1)
            a = nxt("A", 2)
            mm(psA[a][:, :], f"psA{a}", lambda c, p=p: W1b[:, c, p * 128:(p + 1) * 128], lambda c: hT[par][:, c, :],
               ["W1b"] + hnames)
            A("dve", lambda e, a=a, p=p: e.tensor_copy(out=ktsb[par][:, p, :], in_=psA[a][:, :]),
              reads=[f"psA{a}"], writes=[f"ktsb{par}"])
            v_block(p)
            fut_post(4 * m + p)
            if m >= 1 or p >= 1:
                emit_w(max(1, -(-36 // (4 * NG - 4))))
        A("pool", lambda e, m=m: e.dma_start(out=kT_scr[:, :, 16 + 512 * m:16 + 512 * (m + 1)].rearrange("p r t -> r p t"),
                                             in_=ktsb[par][:]),
          reads=[f"ktsb{par}"], writes=["kT_scr"], dma=True)
        A("pool", lambda e, m=m: e.dma_start(out=v_scr[:, :, 1 + 4 * m:5 + 4 * m, :].rearrange("p k b c -> k p b c"),
                                             in_=vsb[par][:].rearrange("k p b h c -> k p b (h c)")),
          reads=[f"vsb{par}"], writes=["v_scr"], dma=True)
        for jj in range(4):
            mm(psS[2][:, jj * 8:(jj + 1) * 8], "psS2", lambda c, jj=jj: hT[par][:, c, jj * 128:(jj + 1) * 128],
               lambda c: W1b[:, c, 1024:1032], ["W1b", f"hT{par}.{jj}"])
        A("dve", lambda e, m=m: e.tensor_tensor(out=Z[:, 1 + 4 * m:5 + 4 * m, :],
                                                in0=psS[2][:, 0:32].rearrange("k (j h) -> k j h", j=4), in1=bft[:], op=ALU.add),
          reads=["psS2", "bft"], writes=["Z"])

    emit_w(len(wq))
    pg.mark("groups")
    sbk = {}

    def sbn_pre(s_, jq):
        blk = 4 * (4 * s_ + jq) + 3
        k = nxt("x", 3)
        A("sp", lambda e: e.dma_start(out=xt[k][:], in_=xa[blk * 128:(blk + 1) * 128, :]), writes=[f"xt{k}"], dma=True)
        return norm_pre(xt[k], f"xt{k}", 128)

    def sbn_post(s_, jq, kx):
        norm_post(kx, 128, hT[0][:, :, jq * 128:(jq + 1) * 128], f"hT0.{jq}", evac_eng="dve")

    for jq in range(4):
        sbn_post(0, jq, sbn_pre(0, jq))
    n_all = NBLK1 * H
    Zf = Z[:].rearrange("p b h -> p (b h)"); LFf = LF[:].rearrange("p b h -> p (b h)")
    CLf = CL[:].rearrange("p b h -> p (b h)"); TOTf = TOT[:].rearrange("p b h -> p (b h)")
    A("act", lambda e: e.activation(out=LFf, in_=Zf, func=AF.Exp, scale=-1.0), reads=["Z"], writes=["LF"])
    A("act", lambda e: e.activation(out=LFf, in_=LFf, func=AF.Ln, bias=onec[:, :]), reads=["LF", "onec"], writes=["LF"])
    A("dve", lambda e: e.tensor_scalar(out=LFf, in0=LFf, scalar1=-1.0, scalar2=None, op0=ALU.mult), reads=["LF"], writes=["LF"])
    A("dve", lambda e: e.tensor_scalar(out=LF[:, 0, :], in0=LF[:, 0, :], scalar1=mrt[:, 0:1], scalar2=None, op0=ALU.mult),
      reads=["LF", "mrt"], writes=["LF"])
    A("pool", lambda e: e.dma_start(out=nlf_own.rearrange("(m t) h -> t m h", t=128),
                                    in_=LF[:, 1:NBLK1, :].rearrange("p (m j) h -> p m j h", j=4)[:, :, 3, :]),
      reads=["LF"], dma=True)
    A("pool", lambda e: e.dma_start(out=nlf_m, in_=LF[0:16, 0, :]), reads=["LF"], dma=True)
    for c0 in range(0, n_all, 512):
        c1 = min(n_all, c0 + 512)
        A("pe", lambda e, c0=c0, c1=c1: e.matmul(psO[0][:, 0:c1 - c0], lhsT=cstt[:, 128:256], rhs=LFf[:, c0:c1], start=True, stop=True),
          reads=["cstt", "LF"], writes=["psO0"])
        A("dve", lambda e, c0=c0, c1=c1: e.tensor_copy(out=CLf[:, c0:c1], in_=psO[0][:, 0:c1 - c0]), reads=["psO0"], writes=["CL"])
        A("pe", lambda e, c0=c0, c1=c1: e.matmul(psO[1][:, 0:c1 - c0], lhsT=onesf[:], rhs=LFf[:, c0:c1], start=True, stop=True),
          reads=["onesf", "LF"], writes=["psO1"])
        A("dve", lambda e, c0=c0, c1=c1: e.tensor_copy(out=TOTf[:, c0:c1], in_=psO[1][:, 0:c1 - c0]), reads=["psO1"], writes=["TOT"])
    T4 = TOT[:, 1:NBLK1, :].rearrange("p (m j) h -> p m j h", j=4)
    BP4 = BP[:, 1:NBLK1, :].rearrange("p (m j) h -> p m j h", j=4)
    CL4 = CL[:, 1:NBLK1, :].rearrange("p (m j) h -> p m j h", j=4)
    A("dve", lambda e: e.tensor_copy(out=sa[0][:, 0, :], in_=TOT[:, 0, :]), reads=["TOT"], writes=["sa0"])
    A("dve", lambda e: e.tensor_tensor(out=sa[0][:, 1:NG + 1, :], in0=T4[:, :, 0, :], in1=T4[:, :, 1, :], op=ALU.add),
      reads=["TOT", "sa0"], writes=["sa0"])
    for j in (2, 3):
        A("dve", lambda e, j=j: e.tensor_tensor(out=sa[0][:, 1:NG + 1, :], in0=sa[0][:, 1:NG + 1, :], in1=T4[:, :, j, :], op=ALU.add),
          reads=["TOT", "sa0"], writes=["sa0"])
    cur = 0
    d = 1
    while d < NG + 1:
        A("dve", lambda e, cur=cur, d=d: e.tensor_tensor(out=sa[1 - cur][:, d:NG + 1, :], in0=sa[cur][:, d:NG + 1, :],
                                                         in1=sa[cur][:, 0:NG + 1 - d, :], op=ALU.add),
          reads=[f"sa{cur}"], writes=[f"sa{1 - cur}"])
        A("dve", lambda e, cur=cur, d=d: e.tensor_copy(out=sa[1 - cur][:, 0:d, :], in_=sa[cur][:, 0:d, :]),
          reads=[f"sa{cur}", f"sa{1 - cur}"], writes=[f"sa{1 - cur}"])
        cur = 1 - cur
        d *= 2
    A("pool", lambda e: e.memset(BP[:, 0, :], 0.0), writes=["BP"])
    for jj in range(4):
        A("dve", lambda e, jj=jj, cur=cur: e.tensor_copy(out=BP4[:, :, jj, :], in_=sa[cur][:, 0:NG, :]),
          reads=[f"sa{cur}", "BP"], writes=["BP"])
        for j2 in range(4):
            if j2 == jj:
                continue
            A("dve", lambda e, jj=jj, j2=j2: e.scalar_tensor_tensor(out=BP4[:, :, jj, :], in0=T4[:, :, j2, :],
                                                                    scalar=w4t[:, j2 * 4 + jj:j2 * 4 + jj + 1],
                                                                    in1=BP4[:, :, jj, :], op0=ALU.mult, op1=ALU.add),
              reads=["TOT", "w4t", "BP"], writes=["BP"])
    A("dve", lambda e: e.tensor_tensor(out=CL[:], in0=CL[:], in1=BP[:], op=ALU.add), reads=["CL", "BP"], writes=["CL"])
    A("dve", lambda e: e.tensor_scalar(out=NC[:], in0=CL[:], scalar1=-1.0, scalar2=None, op0=ALU.mult), reads=["CL"], writes=["NC"])

    def split3(src8, dstCP, nm_src, nm_dst, tmp, nm_tmp):
        A("dve", lambda e: e.tensor_copy(out=dstCP[:, :, :, 0], in_=src8), reads=[nm_src], writes=[nm_dst])
        A("dve", lambda e: e.tensor_tensor(out=tmp, in0=src8, in1=dstCP[:, :, :, 0], op=ALU.subtract), reads=[nm_src, nm_dst], writes=[nm_tmp])
        A("dve", lambda e: e.tensor_copy(out=dstCP[:, :, :, 1], in_=tmp), reads=[nm_tmp, nm_dst], writes=[nm_dst])
        A("dve", lambda e: e.tensor_tensor(out=tmp, in0=tmp, in1=dstCP[:, :, :, 1], op=ALU.subtract), reads=[nm_tmp, nm_dst], writes=[nm_tmp])
        A("dve", lambda e: e.tensor_copy(out=dstCP[:, :, :, 2], in_=tmp), reads=[nm_tmp, nm_dst], writes=[nm_dst])

    A("dve", lambda e: e.tensor_scalar(out=c8[:], in0=CL4[:, :, 3, :], scalar1=8.0, scalar2=None, op0=ALU.mult), reads=["CL"], writes=["c8"])
    split3(c8[:], CPt[:], "c8", "CPt", r1[:], "r1")
    ck8 = wst[0][:, 0:nb8].rearrange("p (b h) -> p b h", h=H)
    ckt = wst[1][:, 0:nb8].rearrange("p (b h) -> p b h", h=H)
    CPk = ktsb[0].rearrange("p a b -> p (a b)")[:, 0:nb8 * 3].rearrange("p (b h r) -> p b h r", h=H, r=3)
    A("dve", lambda e: e.tensor_scalar(out=ck8, in0=NC[:], scalar1=8.0, scalar2=None, op0=ALU.mult), reads=["NC", "wst0"], writes=["wst0"])
    split3(ck8, CPk, "wst0", "ktsb0", ckt, "wst1")
    a = nxt("A", 2)
    A("pe", lambda e: e.matmul(psA[a][0:24, 0:16], lhsT=CPk[0:16, 0, :, :].rearrange("p h r -> p (h r)"), rhs=identb[0:16, 0:16],
                               start=True, stop=True), reads=["ktsb0", "identb"], writes=[f"psA{a}"])
    A("dve", lambda e: e.tensor_copy(out=ktsb[1][0:24, 0, 0:16], in_=psA[a][0:24, 0:16]), reads=[f"psA{a}"], writes=["ktsb1"])
    A("pool", lambda e: e.dma_start(out=kaug_scr[:, 0:16], in_=ktsb[1][0:24, 0, 0:16]), reads=["ktsb1"], writes=["kaug_scr"], dma=True)
    _b5 = [(psA[0], "psA0"), (psA[1], "psA1"), (psS[0], "psS0"), (psS[1], "psS1"), (psO[0], "psO0"), (psO[1], "psO1")]
    for g in range(NG):
        pa, na = _b5[g % 6]
        for j in range(4):
            A("pe", lambda e, j=j: e.matmul(pa[0:24, j * 128:(j + 1) * 128],
                                            lhsT=CPk[:, 1 + 4 * g + j, :, :].rearrange("p h r -> p (h r)"), rhs=identb[:],
                                            start=True, stop=True), reads=["ktsb0", "identb"], writes=[na])
        kk = 1 + g % 3
        A("dve", lambda e: e.tensor_copy(out=ktsb[1][0:24, kk, :], in_=pa[0:24, :]), reads=[na], writes=[f"ktsb1.{kk}"])
        A("pool", lambda e: e.dma_start(out=kaug_scr[:, 16 + 512 * g:16 + 512 * (g + 1)], in_=ktsb[1][0:24, kk, :]),
          reads=[f"ktsb1.{kk}"], writes=["kaug_scr"], dma=True)

    pg.mark("csum")
    fence_t = sb("fence_t", [128, 1])
    A("pool", lambda e: e.memset(fence_t[:], 0.0), reads=[], writes=P1_NAMES + P2_NAMES + ["hT1.0", "hT1.1", "hT1.2", "hT1.3", "pT2"])
    qa_e = qa.rearrange("p (a two) c -> p a two c", two=2)
    A("pool", lambda e: e.memset(qa_e[64:128, :, 0, :], 0.0), writes=["qa"])
    A("pool", lambda e: e.memset(qa_e[64:70, :, 0, :], 1.0), reads=["qa"], writes=["qa"])
    A("pool", lambda e: e.memset(qa_e[0:64, :, 1, :], 0.0), reads=["qa"], writes=["qa"])
    A("pool", lambda e: e.memset(qa_e[0:6, :, 1, :], 1.0), reads=["qa"], writes=["qa"])
    A("pool", lambda e: e.memset(ca[64:128], 0.0), writes=["ca"])
    _b7 = [(psA[0], "psA0"), (psA[1], "psA1"), (psS[0], "psS0"), (psS[1], "psS1"), (psS[2], "psS2"), (psO[0], "psO0"), (psO[1], "psO1")]

    def bank7():
        return _b7[nxt("B", 7)]

    def silu_gate(ps, psn, npart, ncol, out_ap, out_name, other=None, other_name=None):
        A("act", lambda e: e.activation(out=t5[1][0:npart, 0:ncol], in_=ps[0:npart, 0:ncol], func=AF.Tanh, scale=0.5),
          reads=[psn], writes=["t51", "t51b", "t51c"])
        if other is None:
            A("dve", lambda e: e.scalar_tensor_tensor(out=out_ap, in0=t5[1][0:npart, 0:ncol], scalar=1.0, in1=ps[0:npart, 0:ncol],
                                                      op0=ALU.add, op1=ALU.mult),
              reads=[psn, "t51"], writes=[out_name])
        else:
            A("dve", lambda e: e.scalar_tensor_tensor(out=t5[2][0:npart, 0:ncol], in0=t5[1][0:npart, 0:ncol], scalar=1.0,
                                                      in1=ps[0:npart, 0:ncol], op0=ALU.add, op1=ALU.mult),
              reads=[psn, "t51"], writes=["t52"])
            A("dve", lambda e: e.tensor_tensor(out=out_ap, in0=other, in1=t5[2][0:npart, 0:ncol], op=ALU.mult),
              reads=["t52", other_name], writes=[out_name])

    def conv_chunk(j, hsrc, hnames, ncol, u_view, u_in, halo_fn, acc_v, out_ap):
        pa, na = bank7()
        mm(pa[:, 0:ncol], na, lambda c: W2b[:, c, 2048 + 128 * j:2048 + 128 * (j + 1)], hsrc, ["W2b"] + hnames)
        A("act", lambda e: e.activation(out=t5[0][:, 0:ncol], in_=pa[:, 0:ncol], func=AF.Copy), reads=[na], writes=["t50"])
        pb, nb_ = bank7()
        mm(pb[:, 0:ncol], nb_, lambda c: W2b[:, c, 1536 + 128 * j:1536 + 128 * (j + 1)], hsrc, ["W2b"] + hnames)
        nb = u_in.shape[1]
        L = u_in.shape[2]
        A("dve", lambda e: e.tensor_tensor(out=u_in, in0=pb[:, 0:ncol].rearrange("p (b l) -> p b l", b=nb),
                                           in1=t5[0][:, 0:ncol].rearrange("p (b l) -> p b l", b=nb), op=ALU.mult),
          reads=[nb_, "t50"], writes=["ut"])
        halo_fn()
        A("dve", lambda e: e.tensor_scalar(out=acc_v, in0=u_view[:, :, 2:2 + L], scalar1=cwt[:, j, 2:3], scalar2=None, op0=ALU.mult),
          reads=["ut", "cwt"], writes=["acc"])
        for i in (1, 0):
            A("dve", lambda e, i=i: e.scalar_tensor_tensor(out=acc_v, in0=u_view[:, :, i:i + L], scalar=cwt[:, j, i:i + 1], in1=acc_v,
                                                           op0=ALU.mult, op1=ALU.add),
              reads=["ut", "cwt", "acc"], writes=["acc"])
        pc, nc_ = bank7()
        mm(pc[:, 0:ncol], nc_, lambda c: W2b[:, c, 1024 + 128 * j:1024 + 128 * (j + 1)], hsrc, ["W2b"] + hnames)
        A("dve", lambda e: e.tensor_tensor(out=acc[:, 0:ncol], in0=acc[:, 0:ncol], in1=pc[:, 0:ncol], op=ALU.mult),
          reads=[nc_, "acc"], writes=["acc"])
        pd, nd = bank7()
        mm(pd[:, 0:ncol], nd, lambda c: W2b[:, c, 2560 + 128 * j:2560 + 128 * (j + 1)], hsrc, ["W2b"] + hnames)
        silu_gate(pd, nd, 128, ncol, out_ap, "catc", other=acc[:, 0:ncol], other_name="acc")

    def u_last2(hcols, dst_dram):
        mm(psS[0][0:2, :], "psS0", hcols, lambda c: W2b[:, c, 1536:2048], ["W2b", "hTx"])
        mm(psS[1][0:2, :], "psS1", hcols, lambda c: W2b[:, c, 2048:2560], ["W2b", "hTx"])
        A("act", lambda e: e.activation(out=t5[0][0:2, :], in_=psS[1][0:2, :], func=AF.Copy), reads=["psS1"], writes=["t50"])
        A("dve", lambda e: e.tensor_tensor(out=t5[2][0:2, :], in0=psS[0][0:2, :], in1=t5[0][0:2, :], op=ALU.mult), reads=["psS0", "t50"], writes=["t52"])
        A("pool", lambda e: e.dma_start(out=dst_dram, in_=t5[2][0:2, :]), reads=["t52"], dma=True)

    A("sp", lambda e: e.dma_start(out=xt[0][0:NH, :], in_=xh), writes=["xt0"], dma=True)
    norm_T(xt[0], "xt0", NH, hTx[:, :, 0:NH], "hTx")
    for j in range(4):
        a = nxt("A", 2)
        mm(psA[a][:, 0:NH], f"psA{a}", lambda c, j=j: W2b[:, c, 2048 + 128 * j:2048 + 128 * (j + 1)], lambda c: hTx[:, c, 0:NH], ["W2b", "hTx"])
        A("act", lambda e, a=a: e.activation(out=t5[0][:, 0:NH], in_=psA[a][:, 0:NH], func=AF.Copy), reads=[f"psA{a}"], writes=["t50"])
        b = nxt("A", 2)
        mm(psA[b][:, 0:NH], f"psA{b}", lambda c, j=j: W2b[:, c, 1536 + 128 * j:1536 + 128 * (j + 1)], lambda c: hTx[:, c, 0:NH], ["W2b", "hTx"])
        A("dve", lambda e, b=b, j=j: e.tensor_tensor(out=uh[:, j, :], in0=psA[b][:, 0:NH], in1=t5[0][:, 0:NH], op=ALU.mult),
          reads=[f"psA{b}", "t50"], writes=["uh"])
    u_last2(lambda c: hTx[:, c, NH - 2:NH], ncv)

    pg.mark("halo")

    smode = {"on": False}
    pend = []
    hold = []

    def _pairable(a, b, na, nb_):
        return (a["bias"] is None and b["bias"] is None and a["nk"] == 128 and b["nk"] == 128 and a["q0"] == b["q0"] and na == nb_)

    def attn_push(it, qa_t, ncol):
        if hold:
            (a, qa_a, na) = hold.pop()
            if _pairable(a, it, na, ncol) and qa_a is qa_t:
                _unit([a, it], qa_t, ncol)
                return
            _unit([a], qa_a, na)
        if it["bias"] is None and it["nk"] == 128:
            hold.append((it, qa_t, ncol))
        else:
            _unit([it], qa_t, ncol)

    def _unit(items, qa_t, ncol):
        P_ = nxt("S", 2)
        for i, it in enumerate(items):
            nk, q0 = it["nk"], it["q0"]
            ps = psSS[P_][:, i * 512:(i + 1) * 512]
            A("pe", lambda e, it=it, ps=ps, nk=nk, q0=q0: e.matmul(ps[0:nk, q0:ncol], lhsT=it["klhs"], rhs=qa_t[0:128, it["h"], q0:ncol],
                                                                  start=True, stop=(it["mask"] is None)),
              reads=[it["knm"], "qa"], writes=[SNAMES[P_][i]])
            if it["mask"] is not None:
                mk, w = it["mask"]
                A("pe", lambda e, ps=ps, nk=nk, q0=q0, mk=mk, w=w: e.matmul(ps[0:nk, q0:q0 + w], lhsT=identb[0:nk, 0:nk], rhs=mk,
                                                                           start=False, stop=True),
                  reads=["identb", "maskb", "trib"], writes=[SNAMES[P_][i]])
        pend.append((items, P_, ncol))
        if len(pend) > 1:
            attn_pop()

    def attn_pop():
        items, P_, ncol = pend.pop(0)
        n = len(items)
        nk, q0 = items[0]["nk"], items[0]["q0"]
        src = psSS[P_][0:nk, :].rearrange("p (a b) -> p a b", a=2)[:, 0:n, q0:ncol]
        rd = [SNAMES[P_][i] for i in range(n)]
        if smode["on"]:
            Q_ = 0
            hh = nxt("ph", 2)
            pdst = pTT[0][:, hh:hh + 1, :]
            pname = "pT0a" if hh == 0 else "pT0b"
        else:
            Q_ = nxt("p", 2)
            pdst = pTT[Q_]
            pname = PNAMES[Q_]
        dst = pdst[0:nk, 0:n, q0:ncol]
        if items[0]["bias"] is None:
            A("act", lambda e: e.activation(out=dst, in_=src, func=AF.Exp, scale=0.125), reads=rd, writes=[pname])
        else:
            A("act", lambda e: e.activation(out=dst, in_=src, func=AF.Exp, bias=items[0]["bias"], scale=0.125),
              reads=rd + [items[0].get("bnm", "NC")], writes=[pname])
        for i, it in enumerate(items):
            hs = it["hslot"]
            A("pe", lambda e, it=it, i=i, hs=hs: e.matmul(psO[hs][0:65, q0:ncol], lhsT=it["vlhs"], rhs=pdst[0:nk, i, q0:ncol],
                                                         start=it["first"], stop=it["last"]),
              reads=[it["vnm"], pname], writes=[f"psO{hs}"])

    dq = []
    npush = [0]

    def attn_softflush():
        while hold:
            (a, qa_a, na) = hold.pop()
            _unit([a], qa_a, na)

    def attn_flush():
        while hold:
            (a, qa_a, na) = hold.pop()
            _unit([a], qa_a, na)
        while pend:
            attn_pop()

    def epi_bufs(heads):
        bufs = {}
        for i, (hs, h) in enumerate(heads):
            r32 = (t5[2] if i == 0 else acc)
            rh = (rdh_v if i == 0 else ut.rearrange("p a b -> p (a b)")[:, 0:256].bitcast(BF16))
            rl = (rdl_v if i == 0 else ut.rearrange("p a b -> p (a b)")[:, 256:512].bitcast(BF16))
            rbt = t5[0] if i == 0 else t5[1]
            rbn = "t50" if i == 0 else "t51c"
            bufs[hs] = (r32, rh, rl, ("t52" if i == 0 else "acc"), ("t51" if i == 0 else "ut"), ("t51b" if i == 0 else "utb"), rbt, rbn)
        return bufs

    def epi_a(heads, ncol, c0=0):
        bufs = epi_bufs(heads)
        for hs, h in heads:
            r32, rh, rl, n32, nh, nl, rbt, rbn = bufs[hs]
            A("act", lambda e, r32=r32, hs=hs: e.activation(out=r32[64:65, c0:ncol], in_=psO[hs][64:65, c0:ncol], func=AF.Ln),
              reads=[f"psO{hs}"], writes=[n32])
        for hs, h in heads:
            r32, rh, rl, n32, nh, nl, rbt, rbn = bufs[hs]
            A("dve", lambda e, hs=hs, h=h, rbt=rbt: e.tensor_tensor(out=rbt[0:64, c0:ncol], in0=psO[hs][0:64, c0:ncol], in1=sz[0:64, h, c0:ncol], op=ALU.mult),
              reads=[f"psO{hs}", "sz"], writes=[rbn])
        for hs, h in heads:
            r32, rh, rl, n32, nh, nl, rbt, rbn = bufs[hs]
            A("act", lambda e, r32=r32: e.activation(out=r32[64:65, c0:ncol], in_=r32[64:65, c0:ncol], func=AF.Exp, scale=-1.0),
              reads=[n32], writes=[n32])
        for hs, h in heads:
            r32, rh, rl, n32, nh, nl, rbt, rbn = bufs[hs]
            A("dve", lambda e, r32=r32, rh=rh: e.tensor_copy(out=rh[64:65, c0:ncol], in_=r32[64:65, c0:ncol]), reads=[n32], writes=[nh])
        for hs, h in heads:
            r32, rh, rl, n32, nh, nl, rbt, rbn = bufs[hs]
            A("dve", lambda e, r32=r32, rh=rh: e.tensor_tensor(out=r32[64:65, c0:ncol], in0=r32[64:65, c0:ncol], in1=rh[64:65, c0:ncol],
                                                              op=ALU.subtract), reads=[n32, nh], writes=[n32])
        for hs, h in heads:
            r32, rh, rl, n32, nh, nl, rbt, rbn = bufs[hs]
            A("dve", lambda e, r32=r32, rl=rl: e.tensor_copy(out=rl[64:65, c0:ncol], in_=r32[64:65, c0:ncol]), reads=[n32], writes=[nl])

    def epi_b(heads, ncol, c0=0):
        bufs = epi_bufs(heads)
        pbank = {}
        for hs, h in heads:
            r32, rh, rl, n32, nh, nl, rbt, rbn = bufs[hs]
            a = nxt("A", 2)
            pbank[hs] = a
            A("pe", lambda e, a=a, rh=rh: e.matmul(psA[a][0:64, c0:ncol], lhsT=onesb[64:65, 0:64], rhs=rh[64:65, c0:ncol], start=True, stop=False),
              reads=["onesb", nh], writes=[f"psA{a}"])
            A("pe", lambda e, a=a, rl=rl: e.matmul(psA[a][0:64, c0:ncol], lhsT=onesb[64:65, 0:64], rhs=rl[64:65, c0:ncol], start=False, stop=True),
              reads=["onesb", nl], writes=[f"psA{a}"])
        for hs, h in heads:
            r32, rh, rl, n32, nh, nl, rbt, rbn = bufs[hs]
            a = pbank[hs]
            A("dve", lambda e, h=h, rbt=rbt, a=a: e.tensor_tensor(out=ca[0:64, h, c0:ncol], in0=psA[a][0:64, c0:ncol], in1=rbt[0:64, c0:ncol], op=ALU.mult),
              reads=[rbn, f"psA{a}"], writes=["ca"])

    def attn_epilogue_multi(heads, ncol, c0=0):
        epi_a(heads, ncol, c0)
        epi_b(heads, ncol, c0)

    def attn_epilogue(hs, h, ncol, c0=0):
        attn_epilogue_multi([(hs, h)], ncol, c0)

    def out_proj(tok0, nt, xres, xname, dst):
        prs = []
        for hf in range(2):
            pa, na = bank7()
            for h in range(H):
                A("pe", lambda e, h=h, hf=hf: e.matmul(pa[0:nt, :], lhsT=ca[0:128, h, tok0:tok0 + nt],
                                                       rhs=woab[0:128, h, hf * 512:(hf + 1) * 512], start=(h == 0), stop=False),
                  reads=["ca", "woab"], writes=[na])
            for j in range(4):
                A("pe", lambda e, j=j, hf=hf: e.matmul(pa[0:nt, :], lhsT=catc[:, j, tok0:tok0 + nt],
                                                       rhs=wocb[:, j, hf * 512:(hf + 1) * 512], start=False, stop=(j == 3)),
                  reads=["catc", "wocb"], writes=[na])
            prs.append((pa, na))
        final_norm_store(prs, xres, xname, nt, dst)

    for s in range(NSB):
        hn = [f"hT0.{j}" for j in range(4)]
        hsrc = lambda c: hT[0][:, c, :]
        for p in range(4):
            pa, na = bank7()
            mm(pa[:, :], na, lambda c, p=p: W2b[:, c, p * 128:(p + 1) * 128], hsrc, ["W2b"] + hn)
            pg_, ng_ = bank7()
            for jq in range(4):
                A("pe", lambda e, p=p, jq=jq: e.matmul(pg_[64:67, jq * 128:(jq + 1) * 128], lhsT=CPt[:, 4 * s + jq, 2 * p, :],
                                                       rhs=identb[:], start=True, stop=True, tile_position=(0, 64)),
                  reads=["CPt", "identb"], writes=[ng_])
                A("pe", lambda e, p=p, jq=jq: e.matmul(pg_[0:3, jq * 128:(jq + 1) * 128], lhsT=CPt[:, 4 * s + jq, 2 * p + 1, :],
                                                       rhs=identb[:], start=True, stop=True, tile_position=(0, 0)),
                  reads=["CPt", "identb"], writes=[ng_])
            A("dve", lambda e, p=p: e.tensor_copy(out=qa[0:64, 2 * p, :], in_=pa[0:64, :]), reads=[na], writes=["qa"])
            A("dve", lambda e, p=p: e.tensor_copy(out=qa[64:128, 2 * p + 1, :], in_=pa[64:128, :]), reads=[na], writes=["qa"])
            A("dve", lambda e, p=p: e.tensor_copy(out=qa[64:67, 2 * p, :], in_=pg_[64:67, :]), reads=[ng_], writes=["qa"])
            A("dve", lambda e, p=p: e.tensor_copy(out=qa[0:3, 2 * p + 1, :], in_=pg_[0:3, :]), reads=[ng_], writes=["qa"])
        for h in range(H):
            pb, nb_ = bank7()
            mm(pb[0:64, :], nb_, lambda c, h=h: W2b[:, c, 512 + h * 64:512 + (h + 1) * 64], hsrc, ["W2b"] + hn)
            silu_gate(pb, nb_, 64, 512, sz[0:64, h, :], "sz")
        for j in range(4):
            def halo(j=j):
                A("pool", lambda e: e.tensor_copy(out=ut[:, :, 0:2], in_=uh[:, j, 8 * s:8 * s + 8].rearrange("p (b l) -> p b l", b=4)),
                  reads=["uh"], writes=["ut"])
            conv_chunk(j, hsrc, hn, 512, ut[:], ut[:, :, 2:130], halo, acc[:].rearrange("p (b l) -> p b l", b=4), catc[:, j, :])
        for p in range(4):
            NCH = 2 * (s + 1)
            for ch in range(NCH):
                slot = nxt("kb", 2)
                ncols = KCOL if ch == 0 else KCH * 128
                c0 = 0 if ch == 0 else 16 + KCH * 128 * ch
                for h2 in range(2):
                    d0, a0_ = (0, 67) if h2 == 0 else (64, 3)
                    A("sp", lambda e, slot=slot, h2=h2, p=p, c0=c0, ncols=ncols, d0=d0: e.dma_start(
                        out=kbuf[slot][h2][d0:d0 + 64, 0:ncols], in_=kT_scr[p, h2 * 64:(h2 + 1) * 64, c0:c0 + ncols]),
                      reads=["kT_scr"], writes=[f"kb{slot}{h2}"], dma=True)
                    A("sp", lambda e, slot=slot, h2=h2, p=p, c0=c0, ncols=ncols, a0_=a0_: e.dma_start(
                        out=kbuf[slot][h2][a0_:a0_ + 3, 0:ncols], in_=kaug_scr[3 * (2 * p + h2):3 * (2 * p + h2) + 3, c0:c0 + ncols]),
                      reads=["kaug_scr", f"kb{slot}{h2}"], writes=[f"kb{slot}{h2}"], dma=True)
                nvb = KCH + 1 if ch == 0 else KCH
                b0 = 0 if ch == 0 else 1 + KCH * ch
                A("sp", lambda e, slot=slot, p=p, b0=b0, nvb=nvb: e.dma_start(out=vbuf[slot][:, 0:nvb, :], in_=v_scr[p, :, b0:b0 + nvb, :]),
                  reads=["v_scr"], writes=[f"vb{slot}"], dma=True)
                items = []
                for h2 in range(2):
                    h = 2 * p + h2
                    if ch == 0:
                        items.append(dict(nk=16, klhs=kbuf[slot][h2][0:128, 0:16], knm=f"kb{slot}{h2}", q0=0, mask=None,
                                          bias=None, vlhs=vbuf[slot][0:16, 0, h2 * 66:h2 * 66 + 65], vnm=f"vb{slot}",
                                          hslot=h2, h=h, first=True, last=False))
                    for g in range(2):
                        mk_ = 2 * ch + g
                        for jj in range(4):
                            kc0 = (16 if ch == 0 else 0) + g * 512 + jj * 128
                            vb_ = (1 if ch == 0 else 0) + g * 4 + jj
                            q0 = 0 if mk_ < 4 * s else (mk_ - 4 * s) * 128
                            mask = (maskb[:, jj, :], 128) if mk_ >= 4 * s else None
                            items.append(dict(nk=128, klhs=kbuf[slot][h2][0:128, kc0:kc0 + 128], knm=f"kb{slot}{h2}", q0=q0, mask=mask,
                                              bias=None, vlhs=vbuf[slot][:, vb_, h2 * 66:h2 * 66 + 65],
                                              vnm=f"vb{slot}", hslot=h2, h=h, first=False,
                                              last=(ch == NCH - 1 and g == 1 and jj == 3)))
                for it in items:
                    attn_push(it, qa, 512)
                    npush[0] += 1
                    if dq and npush[0] >= dq[0][0]:
                        dq.pop(0)[1]()
            def hidden(p=p, s=s):
                if s + 1 < NSB:
                    if p == 0:
                        sbk[0] = sbn_pre(s + 1, 0); sbk[1] = sbn_pre(s + 1, 1)
                    elif p == 1:
                        sbn_post(s + 1, 0, sbk[0]); sbn_post(s + 1, 1, sbk[1])
                        sbk[2] = sbn_pre(s + 1, 2); sbk[3] = sbn_pre(s + 1, 3)
                    elif p == 2:
                        sbn_post(s + 1, 2, sbk[2]); sbn_post(s + 1, 3, sbk[3])
            hp = [(0, 2 * p), (1, 2 * p + 1)]
            if p < 3:
                attn_flush()
                hidden()
                npush[0] = 0
                dq.append((1, lambda hp=hp: epi_a(hp, 512)))
                dq.append((9, lambda hp=hp: epi_b(hp, 512)))
            else:
                attn_flush()
                epi_a(hp, 512)
                epi_b(hp, 512)
        for jq in range(4):
            mq = 4 * s + jq
            blk = 4 * mq + 3
            k = nxt("x", 3)
            A("sp", lambda e, k=k, blk=blk: e.dma_start(out=xt[k][:], in_=xa[blk * 128:(blk + 1) * 128, :]),
              writes=[f"xt{k}"], dma=True)
            out_proj(jq * 128, 128, xt[k], f"xt{k}", y_own[mq * 128:(mq + 1) * 128, :])
    pg.mark("phase2")
    if do_sample:
        NBS = PB + 1
        smode["on"] = True
        ckb = hT[1]
        vcs = sb("vcs", [128, NBS, H, 66], BF16)
        Zs = sb("Zs", [16, SBC, H]); LFs = sb("LFs", [128, SBC, NBS, H]); CLs = sb("CLs", [128, SBC, NBS, H])
        TOTs = sb("TOTs", [128, SBC, NBS, H]); BPs = sb("BPs", [128, SBC, NBS, H])
        NCs = sb("NCs", [128, SBC, NBS, H])[:] if SBC * NBS > NBLK1 else NC[:, 0:SBC * NBS, :].rearrange("p (b k) h -> p b k h", b=SBC)
        c8s = sb("c8s", [16, SBC, H]); r1s = sb("r1s", [16, SBC, H]); CPs = sb("CPs", [16, SBC, H, 3], BF16)
        sct = sb("sct", [128, 4, SBC, 2]); uts = sb("uts", [128, SBC, 18])
        A("pool", lambda e: e.memset(fence_t[:], 0.0), reads=[], writes=["hT1.0", "hT1.1", "hT1.2", "hT1.3", "pT2", "pT0", "pT0a", "pT0b"] + [f"kb{i}{j}{x}" for i in range(2) for j in range(2) for x in ("", ".aug")])
        A("sp", lambda e: e.dma_start(out=xt[0][0:NS, :], in_=xs), writes=["xt0"], dma=True)
        A("sp", lambda e: e.dma_start(out=sct[:], in_=scT), writes=["sct"], dma=True)
        norm_T(xt[0], "xt0", NS, hTx[:, :, 0:NS], "hTx")
        A("pool", lambda e: e.memset(LFs[:], 0.0), writes=["LFs"])
        A("pool", lambda e: e.memset(vcs[:], 2.0), writes=["vcs"])
        ckbs = [hT[1], hT[0]]
        cknm = [["hT1.0", "hT1.1", "hT1.2", "hT1.3"], ["hT0.0", "hT0.1", "hT0.2", "hT0.3"]]

        def load_k(b):
            for k0 in range(0, PB, 2):
                k = 1 + nxt("x", 2)
                A("sp", lambda e, b=b, k=k, k0=k0: e.dma_start(out=xt[k][:].rearrange("p (a f) -> p a f", a=2),
                                                              in_=cki[b, k0 * 128:(k0 + 2) * 128, :].rearrange("(a t) f -> t a f", t=128)),
                  writes=[f"xt{k}"], dma=True)
                A("dve", lambda e, k=k, k0=k0, b=b: e.tensor_copy(out=ckbs[b % 2][:, k0:k0 + 2, :], in_=xt[k][:].rearrange("p (a f) -> p a f", a=2)),
                  reads=[f"xt{k}"], writes=cknm[b % 2])

        def load_cache(b):
            for k0 in range(0, PB, 2):
                k = 1 + nxt("x", 2)
                A("sp", lambda e, b=b, k=k, k0=k0: e.dma_start(out=xt[k][:].rearrange("p (a f) -> p a f", a=2),
                                                              in_=cvi[b, k0 * 128:(k0 + 2) * 128, :].rearrange("(a t) f -> t a f", t=128)),
                  writes=[f"xt{k}"], dma=True)
                A("dve", lambda e, k=k, k0=k0: e.tensor_copy(out=vcs[:, k0:k0 + 2, :, 0:64],
                                                             in_=xt[k][:].rearrange("p (a h d) -> p a h d", a=2, h=H)),
                  reads=[f"xt{k}", "vcs"], writes=["vcs"])

        for i_ in range(2):
            A("pool", lambda e, i_=i_: e.memset(kbuf[i_][1][64:70, :], 1.0), reads=[f"kb{i_}1"], writes=[f"kb{i_}1", f"kb{i_}1.aug"])
        for b_ in range(SBC):
            load_k(b_)
        load_cache(0)
        for b in range(SBC):
            A("sp", lambda e, b=b: e.dma_start(out=LFs[:, b, 0:PB, :], in_=clf[b].rearrange("(k t) h -> t k h", t=128)),
              reads=["LFs"], writes=["LFs"], dma=True)
        mm(psS[0][0:NS, :], "psS0", lambda c: hTx[:, c, 0:NS], lambda c: W1b[:, c, 0:512], ["W1b", "hTx"])
        A("act", lambda e: e.activation(out=o32[0][0:NS, :], in_=psS[0][0:NS, :], func=AF.Copy), reads=["psS0"], writes=["o320"])
        A("pool", lambda e: e.dma_start(out=nks, in_=o32[0][0:NS, :]), reads=["o320"], dma=True)
        for b in range(SBC):
            mm(psS[2][0:16, 0:8], "psS2", lambda c, b=b: hTx[:, c, b * 16:(b + 1) * 16], lambda c: W1b[:, c, 1024:1032], ["W1b", "hTx"])
            A("dve", lambda e, b=b: e.tensor_tensor(out=Zs[0:16, b, :], in0=psS[2][0:16, 0:8], in1=bft[0:16, 0, :], op=ALU.add),
              reads=["psS2", "bft"], writes=["Zs"])
        A("act", lambda e: e.activation(out=Zs[:], in_=Zs[:], func=AF.Exp, scale=-1.0), reads=["Zs"], writes=["Zs"])
        A("act", lambda e: e.activation(out=Zs[:], in_=Zs[:], func=AF.Ln, bias=onec[0:16, :]), reads=["Zs", "onec"], writes=["Zs"])
        A("dve", lambda e: e.tensor_scalar(out=LFs[0:16, :, PB, :], in0=Zs[:], scalar1=-1.0, scalar2=None, op0=ALU.mult),
          reads=["Zs", "LFs"], writes=["LFs"])
        for b in range(SBC):
            A("pool", lambda e, b=b: e.dma_start(out=nlfs[b * 16:(b + 1) * 16, :], in_=LFs[0:16, b, PB, :]), reads=["LFs"], dma=True)
        ns_all = SBC * NBS * H
        LFsf = LFs[:].rearrange("p b k h -> p (b k h)")
        A("pe", lambda e: e.matmul(psO[0][:, 0:ns_all], lhsT=cstt[:, 128:256], rhs=LFsf, start=True, stop=True),
          reads=["cstt", "LFs"], writes=["psO0"])
        A("dve", lambda e: e.tensor_copy(out=CLs[:].rearrange("p b k h -> p (b k h)"), in_=psO[0][:, 0:ns_all]), reads=["psO0"], writes=["CLs"])
        A("pe", lambda e: e.matmul(psO[1][:, 0:ns_all], lhsT=onesf[:], rhs=LFsf, start=True, stop=True),
          reads=["onesf", "LFs"], writes=["psO1"])
        A("dve", lambda e: e.tensor_copy(out=TOTs[:].rearrange("p b k h -> p (b k h)"), in_=psO[1][:, 0:ns_all]), reads=["psO1"], writes=["TOTs"])
        A("pool", lambda e: e.memset(BPs[:], 0.0), writes=["BPs"])
        for k in range(PB):
            A("dve", lambda e, k=k: e.tensor_tensor(out=BPs[:, :, k + 1, :], in0=BPs[:, :, k, :], in1=TOTs[:, :, k, :], op=ALU.add),
              reads=["BPs", "TOTs"], writes=["BPs"])
        A("dve", lambda e: e.tensor_tensor(out=CLs[:], in0=CLs[:], in1=BPs[:], op=ALU.add), reads=["CLs", "BPs"], writes=["CLs"])
        A("dve", lambda e: e.tensor_scalar(out=NCs, in0=CLs[:], scalar1=-1.0, scalar2=None, op0=ALU.mult), reads=["CLs", "NC"], writes=["NCs", "NC"])
        A("dve", lambda e: e.tensor_scalar(out=c8s[:], in0=CLs[0:16, :, PB, :], scalar1=8.0, scalar2=None, op0=ALU.mult), reads=["CLs"], writes=["c8s"])
        split3(c8s[:], CPs[:], "c8s", "CPs", r1s[:], "r1s")
        nsk = SBC * NBS * H
        ck8s = vbuf[0].rearrange("p a b -> p (a b)").bitcast(F32)[:, 0:nsk].rearrange("p (b k h) -> p b k h", b=SBC, k=NBS)
        ckts = vbuf[0].rearrange("p a b -> p (a b)").bitcast(F32)[:, nsk:2 * nsk].rearrange("p (b k h) -> p b k h", b=SBC, k=NBS)
        CPks = vbuf[1].rearrange("p a b -> p (a b)")[:, 0:3 * nsk].rearrange("p (b k h r) -> p b k h r", b=SBC, k=NBS, h=H)
        A("dve", lambda e: e.tensor_scalar(out=ck8s, in0=NCs, scalar1=8.0, scalar2=None, op0=ALU.mult), reads=["NCs", "NC", "vb0"], writes=["vb0"])
        A("dve", lambda e: e.tensor_copy(out=CPks[:, :, :, :, 0], in_=ck8s), reads=["vb0", "vb1"], writes=["vb1"])
        A("dve", lambda e: e.tensor_tensor(out=ckts, in0=ck8s, in1=CPks[:, :, :, :, 0], op=ALU.subtract), reads=["vb0", "vb1"], writes=["vb0"])
        A("dve", lambda e: e.tensor_copy(out=CPks[:, :, :, :, 1], in_=ckts), reads=["vb0", "vb1"], writes=["vb1"])
        A("dve", lambda e: e.tensor_tensor(out=ckts, in0=ckts, in1=CPks[:, :, :, :, 1], op=ALU.subtract), reads=["vb0", "vb1"], writes=["vb0"])
        A("dve", lambda e: e.tensor_copy(out=CPks[:, :, :, :, 2], in_=ckts), reads=["vb0", "vb1"], writes=["vb1"])
        kst = rr[:].bitcast(BF16)
        for b in range(SBC):
            for g0 in range(0, NBS, 4):
                pa, na = bank7()
                gn = min(4, NBS - g0)
                for k in range(g0, g0 + gn):
                    A("pe", lambda e, k=k, b=b: e.matmul(pa[b * 32:b * 32 + 24, (k - g0) * 128:(k - g0 + 1) * 128],
                                                         lhsT=CPks[:, b, k, :, :].rearrange("p h r -> p (h r)"), rhs=identb[:],
                                                         start=True, stop=True, tile_position=(0, b * 32)),
                      reads=["vb1", "identb"], writes=[na])
                A("dve", lambda e, g0=g0, gn=gn, b=b: e.tensor_copy(out=kst[b * 32:b * 32 + 24, g0 * 128:(g0 + gn) * 128],
                                                                     in_=pa[b * 32:b * 32 + 24, 0:gn * 128]),
                  reads=[na, "rr0", "rr1"], writes=["rr0", "rr1"])
        hsrc_s = lambda c: hTx[:, c, 0:NS]
        A("pool", lambda e: e.memset(qa[64:128, :, 0:NS], 0.0), reads=["qa"], writes=["qa"])
        A("pool", lambda e: e.memset(qa[64:70, :, 0:NS], 1.0), reads=["qa"], writes=["qa"])
        for h in range(H):
            a = nxt("A", 2)
            mm(psA[a][0:64, 0:NS], f"psA{a}", lambda c, h=h: W2b[:, c, h * 64:(h + 1) * 64], hsrc_s, ["W2b", "hTx"])
            for b in range(SBC):
                A("pe", lambda e, a=a, h=h, b=b: e.matmul(psA[a][64:67, b * 16:(b + 1) * 16], lhsT=CPs[0:16, b, h, :],
                                                          rhs=identb[0:16, 0:16], start=True, stop=True, tile_position=(0, 64)),
                  reads=["CPs", "identb"], writes=[f"psA{a}"])
            A("dve", lambda e, a=a, h=h: e.tensor_copy(out=qa[0:67, h, 0:NS], in_=psA[a][0:67, 0:NS]), reads=[f"psA{a}"], writes=["qa"])
            b_ = nxt("A", 2)
            mm(psA[b_][0:64, 0:NS], f"psA{b_}", lambda c, h=h: W2b[:, c, 512 + h * 64:512 + (h + 1) * 64], hsrc_s, ["W2b", "hTx"])
            silu_gate(psA[b_], f"psA{b_}", 64, NS, sz[0:64, h, 0:NS], "sz")
        for j in range(4):
            def halo_s(j=j):
                A("pool", lambda e: e.tensor_copy(out=uts[:, :, 0:2], in_=sct[:, j, :, :]), reads=["sct"], writes=["ut"])
            conv_chunk(j, hsrc_s, ["hTx"], NS, uts[:], uts[:, :, 2:18], halo_s, acc[:, 0:NS].rearrange("p (b l) -> p b l", b=SBC),
                       catc[:, j, 0:NS])
        for b in range(SBC):
            u_last2(lambda c, b=b: hTx[:, c, b * 16 + 14:b * 16 + 16], ncs[b])
        for b in range(SBC):
            if b > 0:
                load_cache(b)
            s_ = nxt("S", 3)
            mm(psS[s_][0:16, :], f"psS{s_}", lambda c, b=b: hTx[:, c, b * 16:(b + 1) * 16], lambda c: W1b[:, c, 512:1024], ["W1b", "hTx"])
            A("dve", lambda e, s_=s_: e.tensor_copy(out=vcs[0:16, PB, :, 0:64], in_=psS[s_][0:16, :].rearrange("k (h d) -> k h d", h=H)),
              reads=[f"psS{s_}", "vcs"], writes=["vcs"])
            o = 0
            A("act", lambda e, s_=s_, o=o: e.activation(out=o32[o][0:16, :], in_=psS[s_][0:16, :], func=AF.Copy), reads=[f"psS{s_}"], writes=[f"o32{o}"])
            A("pool", lambda e, o=o, b=b: e.dma_start(out=nvs[b * 16:(b + 1) * 16, :], in_=o32[o][0:16, :]), reads=[f"o32{o}"], dma=True)
            pend_h = []

            def fin_head(hh, kb2, kn2, P2_, b=b):
                h2_ = hh % 2
                Qh = nxt("ph", 2)
                pd = pTT[0][:, Qh, :]
                pn = "pT0a" if Qh == 0 else "pT0b"
                A("act", lambda e: e.activation(out=pd[0:128, 0:PB * 16], in_=psSS[P2_][0:128, 0:PB * 16], func=AF.Exp, scale=0.125),
                  reads=[SNAMES[P2_][0]], writes=[pn])
                A("act", lambda e: e.activation(out=pd[0:16, PB * 16:NBS * 16], in_=psSS[P2_][0:16, PB * 16:NBS * 16], func=AF.Exp, scale=0.125),
                  reads=[SNAMES[P2_][0], pn], writes=[pn])
                for k in range(NBS):
                    nk = 128 if k < PB else 16
                    A("pe", lambda e, k=k, nk=nk: e.matmul(psO[h2_][0:65, b * 16:(b + 1) * 16], lhsT=vcs[0:nk, k, hh, 0:65],
                                                           rhs=pd[0:nk, k * 16:(k + 1) * 16], start=(k == 0), stop=(k == PB)),
                      reads=["vcs", pn], writes=[f"psO{h2_}"])
                attn_epilogue(h2_, hh, b * 16 + 16, c0=b * 16)

            for h in range(H):
                slot = nxt("kb", 2)
                h2 = h % 2
                kb_ = kbuf[slot][h2]
                kn = f"kb{slot}{h2}"
                A("sp", lambda e, kb_=kb_, h=h: e.dma_start(out=kb_[67:70, 0:P + 16], in_=kst[b * 32 + 3 * h:b * 32 + 3 * h + 3, 0:P + 16]),
                  reads=["rr0", "rr1"], writes=[kn + ".aug"], dma=True)
                for k in range(PB):
                    A("pe", lambda e, k=k, h=h: e.transpose(out=psT[0:64, k, :], in_=ckbs[b % 2][:, k, h * 64:(h + 1) * 64], identity=identb[:]),
                      reads=cknm[b % 2] + ["identb"], writes=["psT"])
                A("dve", lambda e, kb_=kb_: e.tensor_copy(out=kb_[0:64, 0:P].rearrange("p (k t) -> p k t", k=PB), in_=psT[0:64, 0:PB, :]),
                  reads=["psT"], writes=[kn])
                a = nxt("A", 2)
                mm(psA[a][0:64, 0:16], f"psA{a}", lambda c, h=h: W1b[:, c, h * 64:(h + 1) * 64], lambda c, b=b: hTx[:, c, b * 16:(b + 1) * 16],
                   ["W1b", "hTx"])
                A("dve", lambda e, a=a, kb_=kb_: e.tensor_copy(out=kb_[0:64, P:P + 16], in_=psA[a][0:64, 0:16]), reads=[f"psA{a}"], writes=[kn])
                P_ = nxt("S", 2)
                for k in range(NBS):
                    nk = 128 if k < PB else 16
                    A("pe", lambda e, k=k, nk=nk, kb_=kb_, h=h, P_=P_: e.matmul(psSS[P_][0:nk, k * 16:(k + 1) * 16], lhsT=kb_[0:128, k * 128:k * 128 + nk],
                                                                               rhs=qa[0:128, h, b * 16:(b + 1) * 16], start=True, stop=(k < PB)),
                      reads=[kn, kn + ".aug", "qa"], writes=[SNAMES[P_][0]])
                A("pe", lambda e, P_=P_: e.matmul(psSS[P_][0:16, PB * 16:NBS * 16], lhsT=identb[0:16, 0:16], rhs=trib[0:16, 0:16], start=False, stop=True),
                  reads=["identb", "trib"], writes=[SNAMES[P_][0]])
                pend_h.append((h, kb_, kn, P_))
                if len(pend_h) > 1:
                    fin_head(*pend_h.pop(0))
            while pend_h:
                fin_head(*pend_h.pop(0))
        out_proj(0, NS, xt[0], "xt0", ys)

    if upto is not None:
        pg.ops = pg.ops[:pg.marks[upto]]
    pg.emit()
    return nc


_SPL = dict(q=(0, 512), k=(512, 1024), v=(1024, 1536), za=(1536, 2048), fl=(2048, 2056), B=(2056, 2568), C=(2568, 3080),
            hc=(3080, 3592), zc=(3592, 4104))


def _prep_common(norm_g, w_in, b_f, conv_w, w_out, final_g):
    w = np.asarray(w_in[0], np.float32)
    cols = lambda *ks: np.concatenate([w[:, _SPL[k][0]:_SPL[k][1]] for k in ks], axis=1)
    pcn = lambda a: np.ascontiguousarray(a.reshape(KC, 128, a.shape[1]).transpose(1, 0, 2))
    wo = np.asarray(w_out[0], np.float32)
    U = np.triu(np.ones((128, 128), np.float32))
    tri = np.where(np.arange(128)[:, None] <= np.arange(128)[None, :], 0.0, NEG).astype(np.float32)
    mrow = np.zeros((128, 1), np.float32); mrow[:16] = 1.0
    return dict(
        w1=pcn(cols("k", "v", "fl")), w2=pcn(cols("q", "za", "B", "C", "hc", "zc")),
        woa=np.ascontiguousarray(wo[0:512].reshape(H, 64, D).transpose(1, 0, 2)),
        woc=np.ascontiguousarray(wo[512:1024].reshape(4, 128, D).transpose(1, 0, 2)),
        gcol=np.ascontiguousarray(np.asarray(norm_g[0], np.float32).reshape(KC, 128).T),
        fg=np.ascontiguousarray(np.tile(np.asarray(final_g, np.float32)[None, :], (128, 1))),
        bf=np.ascontiguousarray(np.tile(np.asarray(b_f[0], np.float32)[None, None, :], (128, 4, 1))),
        cw=np.ascontiguousarray(np.asarray(conv_w[0], np.float32).reshape(3, 4, 128).transpose(2, 1, 0)),
        cst=np.ascontiguousarray(np.concatenate([np.eye(128, dtype=np.float32), U, tri], axis=1)),
        mrow=mrow)


def _perm(r):
    return [j for j in range(4) if j != r] + [r]


def _core_inputs(common, r, xb, meta_tokens, NB):
    NG = NB // 4
    pm = _perm(r)
    x4 = xb.reshape(NG, 4, 128, D)
    xa = np.ascontiguousarray(x4[:, pm].reshape(NB * 128, D))
    xp = np.concatenate([meta_tokens, xb], axis=0)
    rows = []
    for m in range(NG):
        p0 = 16 + (4 * m + r) * 128
        rows.append(xp[p0 - 2:p0])
    rows.append(xp[-2:])
    xh = np.ascontiguousarray(np.concatenate(rows, axis=0))
    tri = common["cst"][:, 256:384]
    msk = np.zeros((128, 4, 128), np.float32)
    for jj in range(4):
        j = pm[jj]
        if j == r:
            msk[:, jj, :] = tri
        elif j > r:
            msk[:, jj, :] = NEG
    w4 = np.zeros((16,), np.float32)
    for j2 in range(4):
        for jj in range(4):
            w4[j2 * 4 + jj] = 1.0 if pm[j2] < pm[jj] else 0.0
    d = dict(common)
    d.update(xa=xa, meta=np.ascontiguousarray(meta_tokens), xh=xh, msk=np.ascontiguousarray(msk.reshape(128, 512)),
             w4=np.ascontiguousarray(np.tile(w4[None, :], (128, 1))))
    return d


_NC_CACHE = {}


def kernel(x_prompt, x_sample, cache_k, cache_v, cache_logf, state_conv, meta_tokens,
           norm_g, w_in, b_f, conv_w, w_out, final_g, _runner=None, _upto=None):
    f = lambda a: np.asarray(a, np.float32)
    x_prompt, x_sample, cache_k, cache_v, cache_logf, state_conv, meta_tokens = map(
        f, (x_prompt, x_sample, cache_k, cache_v, cache_logf, state_conv, meta_tokens))
    B, SEQ, _ = x_prompt.shape
    NB = SEQ // 128
    NG = NB // 4
    DB, S, _ = x_sample.shape
    P = cache_k.shape[2]
    n_cores = 4 * B
    SBC = DB // n_cores
    key = (NB, SBC, P, _upto)
    if key not in _NC_CACHE:
        _NC_CACHE[key] = build(NB=NB, SBC=SBC, P=P, do_sample=True, upto=_upto)
    nc = _NC_CACHE[key]
    common = _prep_common(f(norm_g), f(w_in), f(b_f), f(conv_w), f(w_out), f(final_g))
    in_maps = []
    for core in range(n_cores):
        b, r = divmod(core, 4)
        d = _core_inputs(common, r, x_prompt[b], meta_tokens, NB)
        sl = slice(core * SBC, (core + 1) * SBC)
        d["xs"] = np.ascontiguousarray(x_sample[sl].reshape(SBC * 16, D))
        d["ck"] = np.ascontiguousarray(cache_k[0, sl].reshape(SBC, P, 512))
        d["cv"] = np.ascontiguousarray(cache_v[0, sl].reshape(SBC, P, 512))
        d["clf"] = np.ascontiguousarray(cache_logf[0, sl])
        d["scT"] = np.ascontiguousarray(state_conv[0, sl].reshape(SBC, 2, 4, 128).transpose(3, 2, 0, 1))
        in_maps.append(d)
    if _runner is not None:
        results = _runner(nc, in_maps)
    else:
        results = run_bass_kernel_spmd(nc, in_maps, core_ids=list(range(n_cores))).results
    L = 16 + SEQ
    y_prompt = np.zeros((B, SEQ, D), np.float32)
    nk = np.zeros((1, B, L, H, 64), np.float32); nv = np.zeros((1, B, L, H, 64), np.float32)
    nlf = np.zeros((1, B, L, H), np.float32); ncp = np.zeros((1, B, 2, 512), np.float32)
    y_sample = np.zeros((DB, S, D), np.float32)
    nks = np.zeros((1, DB, S, H, 64), np.float32); nvs = np.zeros((1, DB, S, H, 64), np.float32)
    nlfs = np.zeros((1, DB, S, H), np.float32); ncs = np.zeros((1, DB, 2, 512), np.float32)
    for core in range(n_cores):
        b, r = divmod(core, 4)
        res = results[core]
        for m in range(NG):
            t0 = (4 * m + r) * 128
            y_prompt[b, t0:t0 + 128] = res["y_own"][m * 128:(m + 1) * 128]
            nk[0, b, 16 + t0:16 + t0 + 128] = res["nk_own"][m * 128:(m + 1) * 128].reshape(128, H, 64)
            nv[0, b, 16 + t0:16 + t0 + 128] = res["nv_own"][m * 128:(m + 1) * 128].reshape(128, H, 64)
            nlf[0, b, 16 + t0:16 + t0 + 128] = res["nlf_own"][m * 128:(m + 1) * 128]
        if r == 0:
            nk[0, b, 0:16] = res["nk_m"].reshape(16, H, 64); nv[0, b, 0:16] = res["nv_m"].reshape(16, H, 64)
            nlf[0, b, 0:16] = res["nlf_m"]; ncp[0, b] = res["ncv"]
        sl = slice(core * SBC, (core + 1) * SBC)
        y_sample[sl] = res["ys"].reshape(SBC, 16, D)
        nks[0, sl] = res["nks"].reshape(SBC, 16, H, 64); nvs[0, sl] = res["nvs"].reshape(SBC, 16, H, 64)
        nlfs[0, sl] = res["nlfs"].reshape(SBC, 16, H); ncs[0, sl] = res["ncs"]
    return (y_prompt, y_sample, nk, nv, nlf, ncp, nks, nvs, nlfs, ncs)
```
